# Optimizing a Trainium2 kernel written in Bass

```python
import math
import jax, jax.numpy as jnp
from jax import lax
import numpy as np

D_MODEL = 1024
BATCH = 4
SEQ = 4096
DEPTH = 1

PLE_DIM = 256
N_HEADS = 4
HEAD_DIM = 64
ATTN_W = N_HEADS * 2 * HEAD_DIM
N_POOL_GROUPS = 4
POOL_WINDOWS = (2, 4, 8, 16)
POOL_CH = 128
POOL_W = N_POOL_GROUPS * POOL_CH
MIX_W = ATTN_W + POOL_W
IN_W = 4 * ATTN_W + 2 * POOL_W
QBLK = 128
ALPHA = (2.0 * DEPTH) ** 0.25
BETA = (8.0 * DEPTH) ** -0.25
LN_EPS = 1e-5

kernel_name = "hybrid_diffattn_multipool_deepnorm"


def lambda_init(layer_idx):
    return 0.8 - 0.6 * math.exp(-0.3 * layer_idx)


def layer_norm(x, g, b):
    xf = x.astype(jnp.float32)
    mu = jnp.mean(xf, axis=-1, keepdims=True)
    var = jnp.mean(jnp.square(xf - mu), axis=-1, keepdims=True)
    y = (xf - mu) * lax.rsqrt(var + LN_EPS)
    return (y * g.astype(jnp.float32) + b.astype(jnp.float32)).astype(x.dtype)


def rms_norm(x, g):
    xf = x.astype(jnp.float32)
    y = xf * lax.rsqrt(jnp.mean(jnp.square(xf), axis=-1, keepdims=True) + LN_EPS)
    return (y * g.astype(jnp.float32)).astype(x.dtype)


def diff_attention(q, k, v, lam):
    B, S, H, _, Dh = q.shape
    nb = S // QBLK
    scale = 1.0 / math.sqrt(Dh)
    qb = q.reshape(B, nb, QBLK, H, 2, Dh).transpose(1, 0, 2, 3, 4, 5)
    kpos = jnp.arange(S)

    def block(args):
        qi, bi = args
        s = jnp.einsum('bqhcd,bkhcd->bhcqk', qi, k).astype(jnp.float32) * scale
        qpos = bi * QBLK + jnp.arange(QBLK)
        causal = kpos[None, :] <= qpos[:, None]
        s = jnp.where(causal, s, -jnp.inf)
        a = jax.nn.softmax(s, axis=-1)
        a = a[:, :, 0] - lam * a[:, :, 1]
        return jnp.einsum('bhqk,bkhe->bqhe', a.astype(v.dtype), v)

    o = lax.map(block, (qb, jnp.arange(nb)))
    return o.transpose(1, 0, 2, 3, 4).reshape(B, S, H, 2 * Dh)


def multiscale_pool(u, w_pool, scale):
    B, S, _ = u.shape
    ug = u.reshape(B, S, N_POOL_GROUPS, POOL_CH)
    c = jnp.cumsum(ug.astype(jnp.float32), axis=1)
    t = jnp.arange(1, S + 1, dtype=jnp.float32)
    means = []
    for g, w in enumerate(POOL_WINDOWS):
        cg = c[:, :, g]
        lag = jnp.pad(cg, ((0, 0), (w, 0), (0, 0)))[:, :S]
        cnt = jnp.minimum(t, float(w))[None, :, None]
        means.append((cg - lag) / cnt)
    mean = jnp.stack(means, axis=2).astype(u.dtype)
    d = mean - ug
    y = jnp.einsum('bsgc,gcd->bsgd', d, w_pool)
    return y.reshape(B, S, POOL_W) * scale


def setup_inputs(seed: int = 0) -> dict:
    key = jax.random.key(seed)
    ks = jax.random.split(key, 16)
    f32 = jnp.float32
    nrm = lambda k, shape, s: (jax.random.normal(k, shape, f32) * s)
    return {
        "x": nrm(ks[0], (BATCH, SEQ, D_MODEL), 1.0),
        "p": nrm(ks[1], (DEPTH, BATCH, SEQ, PLE_DIM), 1.0),
        "w_in": nrm(ks[2], (DEPTH, D_MODEL, IN_W), D_MODEL ** -0.5),
        "lam_q1": nrm(ks[3], (DEPTH, HEAD_DIM), 0.1),
        "lam_k1": nrm(ks[4], (DEPTH, HEAD_DIM), 0.1),
        "lam_q2": nrm(ks[5], (DEPTH, HEAD_DIM), 0.1),
        "lam_k2": nrm(ks[6], (DEPTH, HEAD_DIM), 0.1),
        "subln_g": 1.0 + nrm(ks[7], (DEPTH, 2 * HEAD_DIM), 0.05),
        "w_pool": nrm(ks[8], (DEPTH, N_POOL_GROUPS, POOL_CH, POOL_CH), POOL_CH ** -0.5),
        "pool_scale": 1.0 + nrm(ks[9], (DEPTH, POOL_W), 0.1),
        "w_out": nrm(ks[10], (DEPTH, MIX_W, D_MODEL), MIX_W ** -0.5 * BETA),
        "ln_g": 1.0 + nrm(ks[11], (DEPTH, D_MODEL), 0.05),
        "ln_b": nrm(ks[12], (DEPTH, D_MODEL), 0.02),
        "w_pe": nrm(ks[13], (DEPTH, PLE_DIM, D_MODEL), PLE_DIM ** -0.5),
        "w_pg": nrm(ks[14], (DEPTH, D_MODEL, D_MODEL), D_MODEL ** -0.5),
        "b_pg": nrm(ks[15], (DEPTH, D_MODEL), 0.02),
    }


def reference(x, p, w_in, lam_q1, lam_k1, lam_q2, lam_k2, subln_g, w_pool, pool_scale,
              w_out, ln_g, ln_b, w_pe, w_pg, b_pg):
    B, S, _ = x.shape
    h = x
    for i in range(DEPTH):
        lam_init = lambda_init(i)
        z = h @ w_in[i]
        q = z[..., 0:ATTN_W].reshape(B, S, N_HEADS, 2, HEAD_DIM)
        k = z[..., ATTN_W:2 * ATTN_W].reshape(B, S, N_HEADS, 2, HEAD_DIM)
        v = z[..., 2 * ATTN_W:3 * ATTN_W].reshape(B, S, N_HEADS, 2 * HEAD_DIM)
        g_attn = z[..., 3 * ATTN_W:4 * ATTN_W]
        u = z[..., 4 * ATTN_W:4 * ATTN_W + POOL_W]
        g_pool = z[..., 4 * ATTN_W + POOL_W:]

        lam = (jnp.exp(jnp.sum(lam_q1[i].astype(jnp.float32) * lam_k1[i].astype(jnp.float32)))
               - jnp.exp(jnp.sum(lam_q2[i].astype(jnp.float32) * lam_k2[i].astype(jnp.float32)))
               + lam_init)
        o = diff_attention(q, k, v, lam)
        o = rms_norm(o, subln_g[i]) * (1.0 - lam_init)
        attn_out = o.reshape(B, S, ATTN_W) * jax.nn.silu(g_attn)

        pool_out = multiscale_pool(u, w_pool[i], pool_scale[i]) * jax.nn.silu(g_pool)

        mix = jnp.concatenate([attn_out, pool_out], axis=-1) @ w_out[i]
        h = layer_norm(ALPHA * h + mix, ln_g[i], ln_b[i])

        gate = jax.nn.sigmoid(h @ w_pg[i] + b_pg[i])
        h = h + (p[i] @ w_pe[i]) * gate
    return h
```

```python
import math
from contextlib import ExitStack

import numpy as np
import concourse.bass as bass
import concourse.mybir as mybir
from concourse.bass_utils import run_bass_kernel_spmd

F32 = mybir.dt.float32
BF16 = mybir.dt.bfloat16
AF = mybir.ActivationFunctionType
ALU = mybir.AluOpType
AX = mybir.AxisListType

D_MODEL = 1024
SEQ = 4096
NG = 8
GT = 512
IN_W = 3072
LN_EPS = 1e-5
ALPHA = 2.0 ** 0.25
LAM_INIT = 0.8 - 0.6 * math.exp(0.0)
POOL_W = (2, 4, 8, 16)
OWN = {0: [0, 3, 4, 7], 1: [1, 2, 5, 6]}
MASK_NEG = -30000.0
SIM_LOG = False
FUSE_WAITS = True
_LAST_PROG = [None]


_SNAP = {}


class Stream:
    def __init__(self, name, sem):
        self.name, self.sem, self.cnt, self.ops, self.seen = name, sem, 0, [], {}

    def wait(self, sem, val):
        if self.seen.get(sem.num, 0) >= val:
            return
        self.seen[sem.num] = val
        self.ops.append(("wait", sem, val))
        snap = _SNAP.get((sem.num, val))
        if snap:
            for k, v in snap.items():
                if self.seen.get(k, 0) < v:
                    self.seen[k] = v


class Buf:
    def __init__(self):
        self.w = {}
        self.r = {}


def _merge(d, t):
    sem, val = t
    if d.get(sem.num, (sem, 0))[1] < val:
        d[sem.num] = (sem, val)


class DSem:
    def __init__(self, sem):
        self.sem, self.cnt = sem, 0


class Prog:
    def __init__(self, nc, st):
        self.nc = nc
        mk = lambda n: st.enter_context(nc.semaphore(n))
        self.PE = Stream("pe", mk("s_pe"))
        self.ACT = Stream("act", mk("s_act"))
        self.DVE = Stream("dve", mk("s_dve"))
        self.POOL = Stream("pool", mk("s_pool"))
        self.SP = Stream("sp", mk("s_sp"))
        self.streams = [self.PE, self.ACT, self.DVE, self.POOL, self.SP]
        self._st = st
        self._nds = 0
        self.log = {} if SIM_LOG else None
        _LAST_PROG[0] = self

    def dsem(self):
        self._nds += 1
        return DSem(self._st.enter_context(self.nc.semaphore(f"d{self._nds}")))

    def _deps(self, S, reads, writes, deps):
        ts = list(deps)
        for b in reads:
            ts += list(b.w.values())
        for b in writes:
            ts += list(b.w.values()) + list(b.r.values())
        for sem, val in ts:
            if S is self.PE and sem is self.PE.sem:
                continue
            S.wait(sem, val)

    def _mark(self, t, reads, writes):
        for b in reads:
            _merge(b.r, t)
        for b in writes:
            b.w = {}
            b.r = {}
            _merge(b.w, t)

    def op(self, S, fn, reads=(), writes=(), sig=True, deps=()):
        self._deps(S, reads, writes, deps)
        if sig:
            S.cnt += 1
            t = (S.sem, S.cnt)
            _SNAP[(S.sem.num, S.cnt)] = dict(S.seen)
        else:
            t = (S.sem, S.cnt + 1)
        S.ops.append(("op", fn, sig))
        self._mark(t, reads, writes)
        return t

    def dma(self, Q, out, in_, ds, reads=(), writes=(), deps=()):
        self._deps(Q, reads, writes, deps)
        ds.cnt += 16
        t = (ds.sem, ds.cnt)
        _SNAP[(ds.sem.num, ds.cnt)] = dict(Q.seen)
        Q.ops.append(("dma", out, in_, ds.sem))
        self._mark(t, reads, writes)
        return t

    def barrier(self, extra=()):
        ts = [(S.sem, S.cnt) for S in self.streams if S.cnt > 0] + list(extra)
        for S in self.streams:
            for sem, val in ts:
                if sem is S.sem:
                    continue
                S.wait(sem, val)

    def flush(self):
        nc = self.nc
        if getattr(self, "log", None) is not None:
            for S in self.streams:
                self.log.setdefault(S.name, []).extend(S.ops)

        def run(S, e):
            pend = []
            for o in S.ops:
                if o[0] == "wait":
                    pend.append(o)
                elif o[0] == "op":
                    for w in pend[:-1]:
                        e.wait_ge(w[1], w[2])
                    ins = o[1](e)
                    if pend and FUSE_WAITS:
                        ins._wait_ge(pend[-1][1], pend[-1][2])
                    pend = []
                    if o[2]:
                        ins.then_inc(S.sem, 1)
                else:
                    for w in pend:
                        e.wait_ge(w[1], w[2])
                    pend = []
                    e.dma_start(out=o[1], in_=o[2]).then_inc(o[3], 16)
            for w in pend:
                e.wait_ge(w[1], w[2])
            S.ops = []

        with nc.Block() as block:
            @block.tensor
            def _(e):
                run(self.PE, e)

            @block.scalar
            def _(e):
                run(self.ACT, e)

            @block.vector
            def _(e):
                run(self.DVE, e)

            @block.gpsimd
            def _(e):
                run(self.POOL, e)

            @block.sync
            def _(e):
                run(self.SP, e)


def build_nc():
    _SNAP.clear()
    nc = bass.Bass("TRN2", target_bir_lowering=False)
    din = lambda n, s: nc.dram_tensor(n, s, F32, kind="ExternalInput").ap()
    xT_d = din("xT", [NG, D_MODEL, GT])
    xown_d = din("x_own", [2048, D_MODEL])
    xh_d = din("xh", [D_MODEL, 64])
    pT_d = din("pT", [256, 2048])
    win_d = din("w_in", [D_MODEL, IN_W])
    wpool_d = din("w_pool", [512, 128])
    wout_d = din("w_out", [1024, 1024])
    wpe_d = din("w_pe", [256, 1024])
    wpg_d = din("w_pg", [1024, 1024])
    lamv_d = din("lamv", [1, 256])
    sublng_d = din("subln_g", [128, 1])
    pscale_d = din("pool_scale", [128, 4])
    lng_d = din("ln_g", [1, 1024])
    lnb_d = din("ln_b", [1, 1024])
    bpg_d = din("b_pg", [1, 1024])
    maskb_d = din("maskb", [128, 4])
    cnt_d = din("cnt", [128, 64])
    tri_d = din("tri2", [128, 256])
    ident_d = din("ident", [128, 128])
    out_d = nc.dram_tensor("out", [2048, D_MODEL], F32, kind="ExternalOutput").ap()

    with ExitStack() as top:
        P = Prog(nc, top)
        PE, ACT, DVE, POOL, SP = P.PE, P.ACT, P.DVE, P.POOL, P.SP

        def sb(st, name, shape, dt):
            return st.enter_context(nc.sbuf_tensor("sb_" + name, shape, dt))

        ps = top.enter_context(nc.psum_tensor("psum_all", [128, 8, 512], F32))
        mixT = sb(top, "mixT", [128, 8, 2048], BF16)
        lamv = sb(top, "lamv", [128, 256], F32)
        sublng = sb(top, "sublng", [128, 1], F32)
        pscale = sb(top, "pscale", [128, 4], F32)
        maskb = sb(top, "maskb", [128, 4], F32)
        cnt = sb(top, "cnt", [128, 64], F32)
        tri2 = sb(top, "tri2", [128, 2, 128], BF16)
        ident = sb(top, "ident", [128, 128], BF16)
        identf = sb(top, "identf", [128, 128], F32)
        ones = sb(top, "ones", [128, 128], F32)
        sc = sb(top, "sc", [128, 16], F32)
        ltmp = sb(top, "ltmp", [128, 128], F32)

        B_mix = [[Buf() for _ in range(4)] for _ in range(8)]
        B_const = Buf()
        B_sc = Buf()

        dsc = P.dsem()
        for dst, src in ((lamv[:], lamv_d.partition_broadcast(128)), (sublng[:], sublng_d),
                         (pscale[:], pscale_d), (maskb[:], maskb_d), (cnt[:], cnt_d)):
            P.dma(SP, dst, src, dsc)
        P.dma(SP, identf[:], ident_d, dsc)
        t_const = [(dsc.sem, dsc.cnt)]
        B_ones = Buf()
        P.op(DVE, lambda e: e.memset(ones[:], 1.0), writes=[B_ones])

        B_lt = Buf()
        P.op(DVE, lambda e: e.tensor_tensor(out=ltmp[:, 0:64], in0=lamv[:, 0:64], in1=lamv[:, 64:128], op=ALU.mult),
             writes=[B_lt], deps=t_const)
        P.op(DVE, lambda e: e.tensor_tensor(out=ltmp[:, 64:128], in0=lamv[:, 128:192], in1=lamv[:, 192:256], op=ALU.mult),
             writes=[B_lt], reads=[B_lt])
        P.op(DVE, lambda e: e.reduce_sum(out=sc[:, 0:1], in_=ltmp[:, 0:64], axis=AX.X), reads=[B_lt], writes=[B_sc])
        P.op(DVE, lambda e: e.reduce_sum(out=sc[:, 1:2], in_=ltmp[:, 64:128], axis=AX.X), reads=[B_lt, B_sc], writes=[B_sc])
        P.op(ACT, lambda e: e.activation(out=sc[:, 2:4], in_=sc[:, 0:2], func=AF.Exp), reads=[B_sc], writes=[B_sc])
        P.op(DVE, lambda e: e.tensor_tensor(out=sc[:, 4:5], in0=sc[:, 2:3], in1=sc[:, 3:4], op=ALU.subtract),
             reads=[B_sc], writes=[B_sc])
        P.op(DVE, lambda e: e.tensor_scalar(out=sc[:, 5:6], in0=sc[:, 4:5], scalar1=LAM_INIT, scalar2=-1.0,
                                            op0=ALU.add, op1=ALU.mult), reads=[B_sc], writes=[B_sc])
        P.op(DVE, lambda e: e.tensor_scalar(out=sc[:, 6:7], in0=sublng[:, 0:1],
                                            scalar1=(1.0 - LAM_INIT) * math.sqrt(128.0), scalar2=None, op0=ALU.mult),
             reads=[B_sc], writes=[B_sc], deps=t_const)
        neglam = sc[:, 5:6]
        gsc = sc[:, 6:7]

        with ExitStack() as p12:
            kT = sb(p12, "kT", [128, 4, SEQ], BF16)
            Vt = sb(p12, "Vt", [128, 32, 512], BF16)
            qT = sb(p12, "qT", [128, 4, 2048], BF16)
            B_k = [[Buf() for _ in range(NG)] for _ in range(4)]
            B_v = [Buf() for _ in range(32)]
            B_q = [[Buf() for _ in range(4)] for _ in range(4)]

            with ExitStack() as p1:
                win = sb(p1, "win", [128, 8, IN_W], BF16)
                xb = [sb(p1, f"xb{i}", [128, 8, GT], BF16) for i in range(2)]
                xh = sb(p1, "xh", [128, 8, 64], BF16)
                wpool = sb(p1, "wpool", [128, 4, 128], BF16)
                uh = sb(p1, "uh", [128, 4, 64], F32)
                U = [sb(p1, f"U{i}", [128, 528], F32) for i in range(2)]
                TA = sb(p1, "TA", [128, 528], F32)
                TB = sb(p1, "TB", [128, 528], F32)
                TM = sb(p1, "TM", [128, 512], F32)
                dbf = [sb(p1, f"dbf{i}", [128, 512], BF16) for i in range(4)]
                gps = [sb(p1, f"gps{i}", [128, 512], F32) for i in range(4)]
                B_xb = [Buf(), Buf()]
                B_U = [Buf(), Buf()]
                B_TA, B_TB, B_TM, B_uh = Buf(), Buf(), Buf(), Buf()
                B_d = [Buf() for _ in range(4)]
                B_g = [Buf() for _ in range(4)]
                B_bank = [Buf() for _ in range(8)]
                bank_i = [0]

                def nbank():
                    b = bank_i[0] % 8
                    bank_i[0] += 1
                    return b

                win_v = win_d.rearrange("(kc p) n -> p kc n", p=128)
                B_w = [Buf() for _ in range(6)]
                ds_x = [P.dsem(), P.dsem()]
                P.dma(POOL, xb[0][:], xT_d[0].rearrange("(kc p) t -> p kc t", p=128), ds_x[0], writes=[B_xb[0]])
                B_w1 = [Buf() for _ in range(4)]
                for h in range(4):
                    P.dma(POOL, win[:, :, 512 + h * 128:512 + (h + 1) * 128], win_v[:, :, 512 + h * 128:512 + (h + 1) * 128],
                          P.dsem(), writes=[B_w1[h]])
                dsm = P.dsem()
                for sec in (2, 0, 3, 4, 5):
                    P.dma(POOL, win[:, :, sec * 512:(sec + 1) * 512], win_v[:, :, sec * 512:(sec + 1) * 512],
                          P.dsem(), writes=[B_w[sec]])
                    if sec == 2:
                        P.dma(POOL, xb[1][:], xT_d[1].rearrange("(kc p) t -> p kc t", p=128), ds_x[1], writes=[B_xb[1]])
                    if sec == 4:
                        P.dma(POOL, xh[:], xh_d.rearrange("(kc p) t -> p kc t", p=128), dsm)
                P.dma(POOL, wpool[:], wpool_d.rearrange("(g c) d -> c g d", c=128), dsm)
                t_misc = [(dsm.sem, dsm.cnt)]
                dsc2 = P.dsem()
                P.dma(POOL, tri2[:].rearrange("p c n -> p (c n)"), tri_d, dsc2)
                P.dma(POOL, ident[:], ident_d, dsc2)
                t_const.append((dsc2.sem, dsc2.cnt))

                evac_rr = [0]

                def evac_copy(bank, out_ap, wbufs, scale=None):
                    src = ps[:, bank, :]
                    evac_rr[0] += 1
                    if evac_rr[0] % 2 == 0:
                        if scale is None:
                            P.op(DVE, lambda e: e.tensor_copy(out=out_ap, in_=src), reads=[B_bank[bank]], writes=wbufs)
                        else:
                            P.op(DVE, lambda e: e.tensor_scalar(out=out_ap, in0=src, scalar1=scale, scalar2=None,
                                                                op0=ALU.mult), reads=[B_bank[bank]], writes=wbufs)
                    else:
                        P.op(ACT, lambda e: e.activation(out=out_ap, in_=src, func=AF.Copy,
                                                         scale=(1.0 if scale is None else scale)),
                             reads=[B_bank[bank]], writes=wbufs)

                def proj_chunk(fc, xbuf, bxb):
                    bank = nbank()
                    sec = fc // 4
                    bw = B_w1[fc - 4] if sec == 1 else B_w[sec]
                    for kc in range(8):
                        P.op(PE, lambda e, kc=kc: e.matmul(ps[:, bank, :], win[:, kc, fc * 128:(fc + 1) * 128],
                                                          xbuf[:, kc, :], start=(kc == 0), stop=(kc == 7)),
                             reads=[bw, bxb], writes=[B_bank[bank]], sig=(kc == 7))
                    return bank

                def halo():
                    for p in range(4):
                        bank = nbank()
                        fc = 16 + p
                        for kc in range(8):
                            P.op(PE, lambda e, kc=kc, fc=fc, bank=bank: e.matmul(
                                ps[:, bank, 0:64], win[:, kc, fc * 128:(fc + 1) * 128], xh[:, kc, :],
                                start=(kc == 0), stop=(kc == 7)),
                                reads=[B_w[4]], writes=[B_bank[bank]], sig=(kc == 7), deps=t_misc)
                        P.op(DVE, lambda e, p=p, bank=bank: e.tensor_copy(out=uh[:, p, :], in_=ps[:, bank, 0:64]),
                             reads=[B_bank[bank]], writes=[B_uh])

                deferred = []

                ORDER = [0, 1, 2, 3, 4, 5, 7, 6]

                def do_group(s, pos):
                    if pos == NG - 1:
                        P.op(ACT, lambda e: e.activation(out=sc[:, 8:9], in_=sc[:, 0:1], func=AF.Exp), reads=[B_sc], writes=[B_sc])
                    own = (s % 2 == 1)
                    j = s // 2
                    xbuf, bxb = xb[pos % 2], B_xb[pos % 2]
                    if 1 <= pos and pos + 1 < NG:
                        P.dma(POOL, xb[(pos + 1) % 2][:], xT_d[ORDER[pos + 1]].rearrange("(kc p) t -> p kc t", p=128),
                              ds_x[(pos + 1) % 2], writes=[B_xb[(pos + 1) % 2]])
                    for h in range(4):
                        bank = proj_chunk(4 + h, xbuf, bxb)
                        evac_copy(bank, kT[:, h, s * GT:(s + 1) * GT], [B_k[h][s]])
                    for f in deferred:
                        f()
                    del deferred[:]
                    for tb in range(4):
                        bank = nbank()
                        for kc in range(8):
                            P.op(PE, lambda e, o=ps[:, bank, :], a=xbuf[:, kc, tb * 128:(tb + 1) * 128], b=win[:, kc, 1024:1536],
                                 st_=(kc == 0), sp_=(kc == 7): e.matmul(o, a, b, start=st_, stop=sp_),
                                 reads=[B_w[2], bxb], writes=[B_bank[bank]], sig=(kc == 7))
                        evac_copy(bank, Vt[:, s * 4 + tb, :], [B_v[s * 4 + tb]])
                    if not own:
                        return
                    tok = slice(j * GT, (j + 1) * GT)
                    for h in range(4):
                        bank = proj_chunk(h, xbuf, bxb)
                        evac_copy(bank, qT[:, h, tok], [B_q[h][j]], scale=0.125)
                    for h in range(4):
                        bank = proj_chunk(12 + h, xbuf, bxb)
                        P.op(ACT, lambda e, o=mixT[:, h, tok], i=ps[:, bank, :]: e.activation(out=o, in_=i, func=AF.Silu),
                             reads=[B_bank[bank]], writes=[B_mix[h][j]])
                    if s == 1:
                        halo()
                    for p in range(4):
                        w = POOL_W[p]
                        Ub, bU = U[p % 2], B_U[p % 2]
                        bank = proj_chunk(16 + p, xbuf, bxb)
                        P.op(DVE, lambda e, o=Ub[:, 16:528], i=ps[:, bank, :]: e.tensor_copy(out=o, in_=i),
                             reads=[B_bank[bank]], writes=[bU])
                        P.op(DVE, lambda e, o=Ub[:, 0:16], i=uh[:, p, j * 16:(j + 1) * 16]: e.tensor_copy(out=o, in_=i),
                             reads=[B_uh, bU], writes=[bU])
                        bank = proj_chunk(20 + p, xbuf, bxb)
                        P.op(ACT, lambda e, o=gps[p][:], i=ps[:, bank, :]: e.activation(out=o, in_=i, func=AF.Silu),
                             reads=[B_bank[bank]], writes=[B_g[p]])
                        srcs = [(Ub, bU)]
                        lv = [(TA, B_TA), (TB, B_TB)]
                        for l in range(p + 1):
                            sh = 1 << l
                            lo = (1 << (l + 1)) - 1
                            src, bsrc = srcs[-1]
                            dst, bdst = lv[l % 2]
                            P.op(DVE, lambda e, o=dst[:, lo:528], a=src[:, lo:528], b=src[:, lo - sh:528 - sh]:
                                 e.tensor_tensor(out=o, in0=a, in1=b, op=ALU.add),
                                 reads=[bsrc], writes=[bdst])
                            srcs.append((dst, bdst))
                        S_w, bS = srcs[-1]
                        P.op(DVE, lambda e, o=dbf[p][:], a=S_w[:, 16:528], b=Ub[:, 16:528], w=w:
                             e.scalar_tensor_tensor(out=o, in0=a, scalar=1.0 / w, in1=b, op0=ALU.mult, op1=ALU.subtract),
                             reads=[bS, bU], writes=[B_d[p]])
                        if j == 0:
                            P.op(DVE, lambda e, a=S_w[:, 16:32], b=cnt[:, p * 16:(p + 1) * 16]:
                                 e.tensor_tensor(out=TM[:, 0:16], in0=a, in1=b, op=ALU.mult),
                                 reads=[bS, B_TM], writes=[B_TM], deps=t_const)
                            P.op(DVE, lambda e, o=dbf[p][:, 0:16], b=Ub[:, 16:32]:
                                 e.tensor_tensor(out=o, in0=TM[:, 0:16], in1=b, op=ALU.subtract),
                                 reads=[B_TM, bU, B_d[p]], writes=[B_d[p]])

                        def ymm(p=p, tok=tok, j=j):
                            bank = nbank()
                            P.op(PE, lambda e: e.matmul(ps[:, bank, :], wpool[:, p, :], dbf[p][:], start=True, stop=True),
                                 reads=[B_d[p]], writes=[B_bank[bank]], deps=t_misc)
                            P.op(DVE, lambda e: e.scalar_tensor_tensor(out=mixT[:, 4 + p, tok], in0=ps[:, bank, :],
                                                                       scalar=pscale[:, p:p + 1], in1=gps[p][:],
                                                                       op0=ALU.mult, op1=ALU.mult),
                                 reads=[B_bank[bank], B_g[p]], writes=[B_mix[4 + p][j]], deps=t_const)
                        deferred.append(ymm)

                for pos, s in enumerate(ORDER):
                    do_group(s, pos)
                for f in deferred:
                    f()
                P.barrier()

            pw3 = ExitStack()
            p12.enter_context(pw3)
            wout = sb(pw3, "wout", [128, 8, 1024], BF16)
            wpg = sb(pw3, "wpg", [128, 8, 1024], BF16)
            wpe = sb(pw3, "wpe", [128, 2, 1024], BF16)
            pTb = sb(pw3, "pTb", [128, 2, 2048], BF16)
            st6 = sb(pw3, "st6", [128, 12], F32)
            mv = [sb(pw3, f"mv{i}", [128, 8], F32) for i in range(2)]
            B_wo, B_wg, B_we, B_pT = Buf(), Buf(), Buf(), Buf()
            P.dma(POOL, wout[:], wout_d.rearrange("(kc p) n -> p kc n", p=128), P.dsem(), writes=[B_wo])
            P.dma(POOL, wpg[:], wpg_d.rearrange("(kc p) n -> p kc n", p=128), P.dsem(), writes=[B_wg])
            P.dma(POOL, pTb[:], pT_d.rearrange("(kc p) t -> p kc t", p=128), P.dsem(), writes=[B_pT])
            P.dma(POOL, wpe[:], wpe_d.rearrange("(kc p) n -> p kc n", p=128), P.dsem(), writes=[B_we])

            if True:
                p2 = p12.enter_context(ExitStack())
                NPT = 6
                PT = [sb(p2, f"PT{i}", [128, 2, 512], BF16) for i in range(NPT)]
                Lacc = [sb(p2, f"Lacc{i}", [128, 2, 512], F32) for i in range(2)]
                ones_bf = sb(p2, "ones_bf", [128, 128], BF16)
                Lh = sb(p2, "Lh", [128, 2, 512], BF16)
                Ll = sb(p2, "Ll", [128, 2, 512], BF16)
                sqh = sb(p2, "sqh", [128, 512], BF16)
                sql = sb(p2, "sql", [128, 512], BF16)
                B_Lh, B_Ll, B_sqh, B_sql = Buf(), Buf(), Buf(), Buf()
                o12s = sb(p2, "o12s", [128, 2, 512], F32)
                rr = sb(p2, "rr", [128, 2, 512], F32)
                lnl = rr
                t1 = sb(p2, "t1", [128, 512], F32)
                t2 = sb(p2, "t2", [128, 512], F32)
                oo = sb(p2, "oo", [128, 512], F32)
                sq = sb(p2, "sq", [128, 512], F32)
                rstd = sb(p2, "rstd", [128, 512], F32)
                lnm = rstd
                orn = sb(p2, "orn", [128, 512], F32)
                B_PT = [Buf() for _ in range(NPT)]
                B_L = [Buf(), Buf()]
                B_S = [Buf(), Buf()]
                B_O = Buf()
                B_Lp = Buf()
                B_o1s, B_o2s, B_onesb = Buf(), Buf(), Buf()
                B_lnl, B_rr, B_t1, B_t2, B_oo, B_sq, B_lnm, B_rstd, B_orn = (Buf() for _ in range(9))
                sslot = [0]
                P.op(DVE, lambda e: e.memset(ones_bf[:], 1.0), writes=[B_onesb])

                def nslot():
                    v = sslot[0] % 2
                    sslot[0] += 1
                    return v

                its = []
                for j in range(4):
                    for h in range(4):
                        ng = 2 * j + 2
                        for s in range(ng):
                            for kb in range(4):
                                its.append(dict(j=j, h=h, s=s, kb=kb, first=(s == 0 and kb == 0),
                                                last=(s == ng - 1 and kb == 3), jh=j * 4 + h, li=s * 4 + kb))
                state = {}

                def qk(i):
                    it = its[i]
                    j, h, s, kb = it["j"], it["h"], it["s"], it["kb"]
                    diag = (s == 2 * j + 1)
                    q0 = 128 * kb if diag else 0
                    sl = nslot()
                    key0 = s * GT + kb * 128
                    for c in range(2):
                        P.op(PE, lambda e, c=c: e.matmul(ps[:, 2 * sl + c, q0:512], kT[64 * c:64 * c + 64, h, key0:key0 + 128],
                                                        qT[64 * c:64 * c + 64, h, j * GT + q0:(j + 1) * GT],
                                                        start=True, stop=(not diag)),
                             reads=[B_k[h][s], B_q[h][j]], writes=[B_S[sl]], sig=(c == 1 and not diag))
                    if diag:
                        for c in range(2):
                            P.op(PE, lambda e, c=c: e.matmul(ps[:, 2 * sl + c, q0:q0 + 128], ident[:], tri2[:, c, :],
                                                            start=False, stop=True),
                                 writes=[B_S[sl]], sig=(c == 1), deps=t_const)
                    pi = i % NPT
                    if s == 2 * j:
                        P.op(ACT, lambda e: e.activation(out=PT[pi][:, :, q0:512], in_=ps[:, 2 * sl:2 * sl + 2, q0:512],
                                                         func=AF.Exp, bias=maskb[:, j:j + 1], scale=1.0),
                             reads=[B_S[sl]], writes=[B_PT[pi]], deps=t_const)
                    else:
                        P.op(ACT, lambda e: e.activation(out=PT[pi][:, :, q0:512], in_=ps[:, 2 * sl:2 * sl + 2, q0:512],
                                                         func=AF.Exp),
                             reads=[B_S[sl]], writes=[B_PT[pi]])
                    lb = it["jh"] % 2
                    li = it["li"]
                    on_pe = li >= 3 and ((li % 2 == 1) if j <= 1 else (li % 3 == 2))
                    if on_pe:
                        pass
                    elif it["first"]:
                        P.op(DVE, lambda e: e.tensor_copy(out=Lacc[lb][:], in_=PT[pi][:]), reads=[B_PT[pi]], writes=[B_L[lb]])
                    else:
                        P.op(DVE, lambda e: e.tensor_tensor(out=Lacc[lb][:, :, q0:512], in0=Lacc[lb][:, :, q0:512],
                                                            in1=PT[pi][:, :, q0:512], op=ALU.add),
                             reads=[B_PT[pi], B_L[lb]], writes=[B_L[lb]])
                    if it["last"] and j > 0:
                        P.op(DVE, lambda e: e.tensor_copy(out=Lh[:], in_=Lacc[lb][:]), reads=[B_L[lb]], writes=[B_Lh])
                        P.op(DVE, lambda e: e.tensor_tensor(out=Ll[:], in0=Lacc[lb][:], in1=Lh[:], op=ALU.subtract),
                             reads=[B_L[lb], B_Lh], writes=[B_Ll])
                    state[i] = (q0, pi, on_pe)

                def av(i):
                    it = its[i]
                    j, h, s, kb = it["j"], it["h"], it["s"], it["kb"]
                    q0, pi, on_pe = state.pop(i)
                    for c in range(2):
                        P.op(PE, lambda e, c=c: e.matmul(ps[:, 4 + c, q0:512], Vt[:, s * 4 + kb, h * 128:(h + 1) * 128],
                                                        PT[pi][:, c, q0:512], start=it["first"], stop=it["last"]),
                             reads=[B_PT[pi], B_v[s * 4 + kb]], writes=[B_O], sig=(c == 1))
                    if on_pe:
                        for c in range(2):
                            P.op(PE, lambda e, c=c: e.matmul(ps[:, 6 + c, q0:512], ones_bf[:], PT[pi][:, c, q0:512],
                                                            start=(it["li"] == (3 if j <= 1 else 5)), stop=False),
                                 reads=[B_PT[pi], B_onesb], writes=[B_Lp], sig=(c == 1))

                def fin0(jh):
                    P.op(ACT, lambda e: e.activation(out=o12s[:], in_=ps[:, 4:6, :], func=AF.Copy), reads=[B_O], writes=[B_o1s, B_o2s])

                def fin1a(jh):
                    if jh < 4:
                        lb = jh % 2
                        for c in range(2):
                            P.op(PE, lambda e, c=c: e.matmul(ps[:, 6 + c, :], ones[:], Lacc[lb][:, c, :], start=False, stop=True),
                                 reads=[B_L[lb], B_ones], writes=[B_Lp], sig=(c == 1))
                        return
                    for c in range(2):
                        P.op(PE, lambda e, c=c: e.matmul(ps[:, 6 + c, :], ones_bf[:], Lh[:, c, :], start=False, stop=False),
                             reads=[B_Lh, B_onesb], writes=[B_Lp], sig=False)
                        P.op(PE, lambda e, c=c: e.matmul(ps[:, 6 + c, :], ones_bf[:], Ll[:, c, :], start=False, stop=True),
                             reads=[B_Ll, B_onesb], writes=[B_Lp], sig=(c == 1))

                def fin1b0(jh):
                    P.op(ACT, lambda e: e.activation(out=lnl[:], in_=ps[:, 6:8, :], func=AF.Ln),
                         reads=[B_Lp], writes=[B_lnl, B_rr])

                def fin1b(jh):
                    P.op(ACT, lambda e: e.activation(out=rr[:], in_=lnl[:], func=AF.Exp, scale=-1.0),
                         reads=[B_lnl], writes=[B_rr, B_lnl])
                    P.op(DVE, lambda e: e.scalar_tensor_tensor(out=t2[:], in0=o12s[:, 1, :], scalar=neglam,
                                                               in1=rr[:, 1, :], op0=ALU.mult, op1=ALU.mult),
                         reads=[B_o2s, B_rr, B_sc], writes=[B_t2])
                    P.op(DVE, lambda e: e.tensor_tensor(out=t1[:], in0=o12s[:, 0, :], in1=rr[:, 0, :], op=ALU.mult),
                         reads=[B_o1s, B_rr], writes=[B_t1])
                    P.op(DVE, lambda e: e.tensor_tensor(out=oo[:], in0=t1[:], in1=t2[:], op=ALU.add),
                         reads=[B_t1, B_t2], writes=[B_oo])
                    P.op(DVE, lambda e: e.tensor_tensor(out=sq[:], in0=oo[:], in1=oo[:], op=ALU.mult),
                         reads=[B_oo], writes=[B_sq])
                    P.op(DVE, lambda e: e.tensor_copy(out=sqh[:], in_=sq[:]), reads=[B_sq], writes=[B_sqh])
                    P.op(DVE, lambda e: e.tensor_tensor(out=sql[:], in0=sq[:], in1=sqh[:], op=ALU.subtract),
                         reads=[B_sq, B_sqh], writes=[B_sql])

                def fin2(jh):
                    j, h = jh // 4, jh % 4
                    tok = slice(j * GT, (j + 1) * GT)
                    sl = nslot()
                    P.op(PE, lambda e: e.matmul(ps[:, 2 * sl, :], ones_bf[:], sqh[:], start=True, stop=False),
                         reads=[B_sqh, B_onesb], writes=[B_S[sl]], sig=False)
                    P.op(PE, lambda e: e.matmul(ps[:, 2 * sl, :], ones_bf[:], sql[:], start=False, stop=True),
                         reads=[B_sql, B_onesb], writes=[B_S[sl]], sig=True)
                    P.op(ACT, lambda e: e.activation(out=lnm[:], in_=ps[:, 2 * sl, :], func=AF.Ln, bias=epsb, scale=1.0),
                         reads=[B_S[sl], B_eps], writes=[B_lnm, B_rstd])
                    P.op(ACT, lambda e: e.activation(out=rstd[:], in_=lnm[:], func=AF.Exp, scale=-0.5),
                         reads=[B_lnm], writes=[B_rstd, B_lnm])
                    P.op(DVE, lambda e: e.tensor_tensor(out=orn[:], in0=oo[:], in1=rstd[:], op=ALU.mult),
                         reads=[B_oo, B_rstd], writes=[B_orn])
                    P.op(DVE, lambda e: e.scalar_tensor_tensor(out=mixT[:, h, tok], in0=orn[:], scalar=gsc,
                                                               in1=mixT[:, h, tok], op0=ALU.mult, op1=ALU.mult),
                         reads=[B_orn, B_sc, B_mix[h][j]], writes=[B_mix[h][j]])

                epsb = sc[:, 7:8]
                B_eps = Buf()
                P.op(DVE, lambda e: e.memset(sc[:, 7:8], 128.0 * LN_EPS), writes=[B_eps])

                class _V:
                    def __init__(self, ap):
                        self.ap = ap

                    def __getitem__(self, k):
                        return self.ap[k]

                def vt_f32(k):
                    return _V(Vt[:, 4 * k:4 * k + 4, :].rearrange("p a n -> p (a n)").bitcast(F32).rearrange("p (a n) -> p a n", a=2))

                def q_f32(h):
                    return _V(qT[:, h, :].bitcast(F32).rearrange("p (a n) -> p a n", a=2))

                def k_f32(h, half):
                    return _V(kT[:, h, half * 2048:(half + 1) * 2048].bitcast(F32).rearrange("p (a n) -> p a n", a=2))

                lngb, lnbb, bpgb = k_f32(0, 0), k_f32(0, 1), k_f32(1, 0)
                xt = [k_f32(1, 1), k_f32(2, 0)]
                yy, yn = vt_f32(2), vt_f32(3)
                hh = [vt_f32(4), vt_f32(5)]
                gsum, gate = vt_f32(6), vt_f32(7)
                pg = q_f32(0)
                ot = [q_f32(1), q_f32(2)]
                hT = [_V(kT[:, 2, 2048:3072]), _V(kT[:, 2, 3072:4096])]
                B_xt = [Buf(), Buf()]
                B_hh = [Buf(), Buf()]
                B_hbf2 = [Buf(), Buf()]
                B_hT = [[Buf(), Buf()], [Buf(), Buf()]]
                B_ot = [Buf(), Buf()]
                B_mv = [Buf(), Buf()]
                B_rs = [Buf(), Buf()]
                B_mc = [Buf(), Buf()]
                B_yy, B_yn, B_gsum, B_gate, B_pg, B_st6 = (Buf() for _ in range(6))
                B_pm, B_pt, B_pgt, B_ppe = B_S[0], B_S[1], B_O, B_Lp
                t_alias = []

                dsv = P.dsem()
                t_vec = []
                ds_xt = [P.dsem(), P.dsem()]
                ds_ot = [P.dsem(), P.dsem()]
                NTB = 16

                def load_x(tb, deps=None):
                    P.dma(SP, xt[tb % 2][:].rearrange("p a n -> p (a n)"), xown_d[tb * 128:(tb + 1) * 128, :], ds_xt[tb % 2],
                          writes=[B_xt[tb % 2]], deps=(t_alias if deps is None else deps))

                for i in range(2):
                    P.op(DVE, lambda e, i=i: e.memset(mv[i][:, 5:6], -0.5), writes=[B_mc[i]])

                def tsl(t):
                    return slice(t * 128, (t + 1) * 128)

                def PE_M(t):
                    for nh in range(2):
                        for kc in range(8):
                            P.op(PE, lambda e, nh=nh, kc=kc: e.matmul(ps[:, nh, :], mixT[:, kc, tsl(t)], wout[:, kc, nh * 512:(nh + 1) * 512],
                                                                     start=(kc == 0), stop=(kc == 7)),
                                 reads=[B_wo, B_mix[kc][t // 4]], writes=[B_pm], sig=(kc == 7 and nh == 1))

                def DVE_Y(t):
                    P.op(DVE, lambda e: e.scalar_tensor_tensor(out=yy[:], in0=xt[t % 2][:], scalar=ALPHA, in1=ps[:, 0:2, :],
                                                               op0=ALU.mult, op1=ALU.add),
                         reads=[B_xt[t % 2], B_pm], writes=[B_yy], deps=t_alias)
                    if t + 2 < NTB:
                        load_x(t + 2)
                    P.op(DVE, lambda e: e.bn_stats(out=st6[:, 0:6], in_=yy[:, 0, :]), reads=[B_yy], writes=[B_st6])
                    P.op(DVE, lambda e: e.bn_stats(out=st6[:, 6:12], in_=yy[:, 1, :]), reads=[B_yy, B_st6], writes=[B_st6])
                    P.op(DVE, lambda e: e.bn_aggr(out=mv[t % 2][:, 0:2], in_=st6[:, 0:12]), reads=[B_st6, B_mv[t % 2]], writes=[B_mv[t % 2]])
                    P.op(DVE, lambda e: e.scalar_tensor_tensor(out=yn[:], in0=yy[:], scalar=mv[t % 2][:, 0:1], in1=lngb[:],
                                                               op0=ALU.subtract, op1=ALU.mult),
                         reads=[B_yy, B_mv[t % 2]], writes=[B_yn], deps=t_vec + t_alias)

                def POOLW3(t):
                    m = mv[t % 2]
                    P.op(POOL, lambda e: e.tensor_scalar(out=m[:, 2:3], in0=m[:, 1:2], scalar1=LN_EPS, scalar2=None, op0=ALU.add),
                         reads=[B_mv[t % 2]], writes=[B_rs[t % 2]])
                    P.op(POOL, lambda e: e.tensor_tensor(out=m[:, 3:4], in0=m[:, 2:3], in1=m[:, 5:6], op=ALU.pow),
                         reads=[B_rs[t % 2], B_mc[t % 2]], writes=[B_rs[t % 2]])

                def DVE_H(t):
                    m = mv[t % 2]
                    P.op(DVE, lambda e: e.scalar_tensor_tensor(out=hh[t % 2][:], in0=yn[:], scalar=m[:, 3:4], in1=lnbb[:],
                                                               op0=ALU.mult, op1=ALU.add),
                         reads=[B_yn, B_rs[t % 2]], writes=[B_hh[t % 2]], deps=t_alias)

                psT = ps[:, 2:4, :]

                def PE_T(t):
                    hflat = hh[t % 2][:].rearrange("p a n -> p (a n)")
                    for kc in range(8):
                        P.op(PE, lambda e, kc=kc: e.transpose(ps[:, 2 + kc // 4, (kc % 4) * 128:(kc % 4 + 1) * 128],
                                                              hflat[:, kc * 128:(kc + 1) * 128], identf[:]),
                             reads=[B_hh[t % 2]], writes=[B_pt], sig=(kc == 7), deps=t_const)

                def ACT_E(t):
                    for hf in range(2):
                        P.op(ACT, lambda e, hf=hf: e.activation(out=hT[t % 2][:, hf * 512:(hf + 1) * 512], in_=ps[:, 2 + hf, :], func=AF.Copy),
                             reads=[B_pt], writes=[B_hT[t % 2][hf]], deps=t_alias)

                def PE_pe(t):
                    for nh in range(2):
                        for kc in range(2):
                            P.op(PE, lambda e, nh=nh, kc=kc: e.matmul(ps[:, 6 + nh, :], pTb[:, kc, tsl(t)], wpe[:, kc, nh * 512:(nh + 1) * 512],
                                                                     start=(kc == 0), stop=(kc == 1)),
                                 reads=[B_we, B_pT], writes=[B_ppe], sig=(kc == 1 and nh == 1))

                def PE_G(t):
                    hTb, bT = hT[t % 2], B_hT[t % 2]
                    PE_pe(t)
                    if t == NTB - 1:
                        PE_pe(t)
                        PE_pe(t)
                    for hf in range(2):
                        for nh in range(2):
                            for kc in range(4 * hf, 4 * hf + 4):
                                P.op(PE, lambda e, nh=nh, kc=kc: e.matmul(ps[:, 4 + nh, :], hTb[:, kc * 128:(kc + 1) * 128],
                                                                         wpg[:, kc, nh * 512:(nh + 1) * 512],
                                                                         start=(kc == 0), stop=(kc == 7)),
                                     reads=[B_wg, bT[hf]], writes=[B_pgt], sig=(kc == 7 and nh == 1))

                def DVE_S(t):
                    P.op(DVE, lambda e: e.tensor_tensor(out=gsum[:], in0=ps[:, 4:6, :], in1=bpgb[:], op=ALU.add),
                         reads=[B_pgt], writes=[B_gsum], deps=t_vec + t_alias)

                def ACT_Z(t):
                    P.op(ACT, lambda e: e.activation(out=gate[:], in_=gsum[:], func=AF.Sigmoid), reads=[B_gsum], writes=[B_gate], deps=t_alias)

                def DVE_Q(t):
                    P.op(DVE, lambda e: e.tensor_tensor(out=pg[:], in0=ps[:, 6:8, :], in1=gate[:], op=ALU.mult),
                         reads=[B_ppe, B_gate], writes=[B_pg], deps=t_alias)

                def POOL_O(t):
                    P.op(DVE if t == NTB - 1 else POOL, lambda e: e.tensor_tensor(out=ot[t % 2][:], in0=pg[:], in1=hh[t % 2][:], op=ALU.add),
                         reads=[B_pg, B_hh[t % 2]], writes=[B_ot[t % 2]], deps=t_alias)
                    P.dma(SP, out_d[tsl(t), :], ot[t % 2][:].rearrange("p a n -> p (a n)"), ds_ot[t % 2], reads=[B_ot[t % 2]])


                n = len(its)
                sched = {}

                def at(step, f):
                    sched.setdefault(step, []).append(f)

                step = 0
                while step <= n + 1:
                    if step < n:
                        qk(step)
                        if step == n - 33:
                            t_early = [(PE.sem, PE.cnt)]
                            for dst, src in ((lngb, lng_d), (lnbb, lnb_d), (bpgb, bpg_d)):
                                P.dma(SP, dst[:].rearrange("p a n -> p (a n)"), src.partition_broadcast(128), dsv, deps=t_early)
                            t_vec.append((dsv.sem, dsv.cnt))
                            load_x(0, t_early)
                            load_x(1, t_early)
                    for f in sched.pop(step, []):
                        f()
                    if 2 <= step <= n + 1:
                        av(step - 2)
                        if its[step - 2]["last"]:
                            jh = its[step - 2]["jh"]
                            fin0(jh)
                            if step <= n:
                                if jh < 4:
                                    at(step + 1, lambda jh=jh: fin1a(jh))
                                    at(step + 2, lambda jh=jh: fin1b0(jh))
                                    at(step + 3, lambda jh=jh: fin1b(jh))
                                    at(step + 8, lambda jh=jh: fin2(jh))
                                else:
                                    fin1a(jh)
                                    at(step + 1, lambda jh=jh: fin1b0(jh))
                                    at(step + 2, lambda jh=jh: fin1b(jh))
                                    at(step + 8, lambda jh=jh: fin2(jh))
                    step += 1
                assert not sched, sorted(sched)
                t_alias.append((PE.sem, PE.cnt))
                PE_M(0); DVE_Y(0); POOLW3(0); DVE_H(0)
                fin1a(15)
                fin1b0(15)
                fin1b(15)
                PE_M(1); PE_T(0); ACT_E(0); DVE_Y(1); POOLW3(1); PE_G(0)
                fin2(15)
                DVE_S(0); ACT_Z(0); DVE_H(1); DVE_Q(0); POOL_O(0)

            if True:
                for t in range(1, NTB):
                    nx = t + 1 < NTB
                    if nx:
                        PE_M(t + 1)
                    PE_T(t)
                    ACT_E(t)
                    if nx:
                        DVE_Y(t + 1)
                        POOLW3(t + 1)
                    PE_G(t)
                    DVE_S(t)
                    ACT_Z(t)
                    if nx:
                        DVE_H(t + 1)
                    DVE_Q(t)
                    POOL_O(t)
                for d in ds_ot:
                    SP.wait(d.sem, d.cnt)
                P.barrier([(d.sem, d.cnt) for d in ds_ot])
                P.flush()
    return nc


_NC_CACHE = {}


def _host_inputs(x, p, w_in, lam_q1, lam_k1, lam_q2, lam_k2, subln_g, w_pool, pool_scale,
                 w_out, ln_g, ln_b, w_pe, w_pg, b_pg):
    f = lambda a: np.ascontiguousarray(np.asarray(a, dtype=np.float32))
    x, p = f(x), f(p)
    common = {
        "w_in": f(w_in)[0], "w_pool": f(w_pool)[0].reshape(512, 128), "w_out": f(w_out)[0],
        "w_pe": f(w_pe)[0], "w_pg": f(w_pg)[0],
        "lamv": np.concatenate([f(lam_q1)[0], f(lam_k1)[0], f(lam_q2)[0], f(lam_k2)[0]]).reshape(1, 256),
        "subln_g": f(subln_g)[0].reshape(128, 1),
        "pool_scale": np.ascontiguousarray(f(pool_scale)[0].reshape(4, 128).T),
        "ln_g": f(ln_g)[0].reshape(1, 1024), "ln_b": f(ln_b)[0].reshape(1, 1024), "b_pg": f(b_pg)[0].reshape(1, 1024),
        "ident": np.eye(128, dtype=np.float32),
    }
    kk = np.arange(128)[:, None]
    qq = np.arange(128)[None, :]
    tri = np.where(qq >= kk, 0.0, MASK_NEG).astype(np.float32)
    common["tri2"] = np.ascontiguousarray(np.concatenate([tri, tri], axis=1))
    in_maps = []
    for c in range(8):
        b, r = c // 2, c % 2
        O = OWN[r]
        X = OWN[1 - r]
        order = [X[0], O[0], X[1], O[1], X[2], O[2], X[3], O[3]]
        xg = x[b].reshape(NG, GT, D_MODEL)
        xT = np.ascontiguousarray(xg[order].transpose(0, 2, 1))
        x_own = np.ascontiguousarray(xg[O].reshape(4 * GT, D_MODEL))
        xh = np.zeros((4, 16, D_MODEL), np.float32)
        for j, g in enumerate(O):
            if g > 0:
                xh[j] = x[b, g * GT - 16:g * GT, :]
        xh = np.ascontiguousarray(xh.reshape(64, D_MODEL).T)
        pT = np.ascontiguousarray(p[0, b].reshape(NG, GT, 256)[O].reshape(4 * GT, 256).T)
        maskb = np.zeros((128, 4), np.float32)
        for j in range(4):
            if (r == 0 and j % 2 == 0) or (r == 1 and j % 2 == 1):
                maskb[:, j] = MASK_NEG
        cnt = np.zeros((128, 4, 16), np.float32)
        t = np.arange(16, dtype=np.float32)
        for pi, w in enumerate(POOL_W):
            if O[0] == 0:
                cnt[:, pi, :] = 1.0 / np.minimum(t + 1.0, float(w))
            else:
                cnt[:, pi, :] = 1.0 / float(w)
        m = dict(common)
        m.update({"xT": xT, "x_own": x_own, "xh": xh, "pT": pT, "maskb": maskb, "cnt": cnt.reshape(128, 64)})
        in_maps.append(m)
    return in_maps


def kernel(**inputs):
    in_maps = _host_inputs(**inputs)
    if "nc" not in _NC_CACHE:
        _NC_CACHE["nc"] = build_nc()
    nc = _NC_CACHE["nc"]
    res = run_bass_kernel_spmd(nc, in_maps, core_ids=list(range(8)))
    out = np.empty((4, SEQ, D_MODEL), np.float32)
    for c in range(8):
        b, r = c // 2, c % 2
        o = np.asarray(res.results[c]["out"], dtype=np.float32).reshape(4, GT, D_MODEL)
        for j, g in enumerate(OWN[r]):
            out[b, g * GT:(g + 1) * GT, :] = o[j]
    return out
```

```python
import math
from contextlib import ExitStack

import numpy as np
import concourse.bass as bass
import concourse.mybir as mybir
from concourse.bass_utils import run_bass_kernel_spmd

F32 = mybir.dt.float32
BF16 = mybir.dt.bfloat16
AF = mybir.ActivationFunctionType
ALU = mybir.AluOpType
AX = mybir.AxisListType

D_MODEL = 1024
SEQ = 4096
NG = 8
GT = 512
IN_W = 3072
LN_EPS = 1e-5
ALPHA = 2.0 ** 0.25
LAM_INIT = 0.8 - 0.6 * math.exp(0.0)
POOL_W = (2, 4, 8, 16)
OWN = {0: [0, 3, 4, 7], 1: [1, 2, 5, 6]}
MASK_NEG = -30000.0
SIM_LOG = False
FUSE_WAITS = True
_LAST_PROG = [None]


_SNAP = {}


class Stream:
    def __init__(self, name, sem):
        self.name, self.sem, self.cnt, self.ops, self.seen = name, sem, 0, [], {}

    def wait(self, sem, val):
        if self.seen.get(sem.num, 0) >= val:
            return
        self.seen[sem.num] = val
        self.ops.append(("wait", sem, val))
        snap = _SNAP.get((sem.num, val))
        if snap:
            for k, v in snap.items():
                if self.seen.get(k, 0) < v:
                    self.seen[k] = v


class Buf:
    def __init__(self):
        self.w = {}
        self.r = {}


def _merge(d, t):
    sem, val = t
    if d.get(sem.num, (sem, 0))[1] < val:
        d[sem.num] = (sem, val)


class DSem:
    def __init__(self, sem):
        self.sem, self.cnt = sem, 0


class Prog:
    def __init__(self, nc, st):
        self.nc = nc
        mk = lambda n: st.enter_context(nc.semaphore(n))
        self.PE = Stream("pe", mk("s_pe"))
        self.ACT = Stream("act", mk("s_act"))
        self.DVE = Stream("dve", mk("s_dve"))
        self.POOL = Stream("pool", mk("s_pool"))
        self.SP = Stream("sp", mk("s_sp"))
        self.streams = [self.PE, self.ACT, self.DVE, self.POOL, self.SP]
        self._st = st
        self._nds = 0
        self.log = {} if SIM_LOG else None
        _LAST_PROG[0] = self

    def dsem(self):
        self._nds += 1
        return DSem(self._st.enter_context(self.nc.semaphore(f"d{self._nds}")))

    def _deps(self, S, reads, writes, deps):
        ts = list(deps)
        for b in reads:
            ts += list(b.w.values())
        for b in writes:
            ts += list(b.w.values()) + list(b.r.values())
        for sem, val in ts:
            if S is self.PE and sem is self.PE.sem:
                continue
            S.wait(sem, val)

    def _mark(self, t, reads, writes):
        for b in reads:
            _merge(b.r, t)
        for b in writes:
            b.w = {}
            b.r = {}
            _merge(b.w, t)

    def op(self, S, fn, reads=(), writes=(), sig=True, deps=()):
        self._deps(S, reads, writes, deps)
        if sig:
            S.cnt += 1
            t = (S.sem, S.cnt)
            _SNAP[(S.sem.num, S.cnt)] = dict(S.seen)
        else:
            t = (S.sem, S.cnt + 1)
        S.ops.append(("op", fn, sig))
        self._mark(t, reads, writes)
        return t

    def dma(self, Q, out, in_, ds, reads=(), writes=(), deps=()):
        self._deps(Q, reads, writes, deps)
        ds.cnt += 16
        t = (ds.sem, ds.cnt)
        _SNAP[(ds.sem.num, ds.cnt)] = dict(Q.seen)
        Q.ops.append(("dma", out, in_, ds.sem))
        self._mark(t, reads, writes)
        return t

    def barrier(self, extra=()):
        ts = [(S.sem, S.cnt) for S in self.streams if S.cnt > 0] + list(extra)
        for S in self.streams:
            for sem, val in ts:
                if sem is S.sem:
                    continue
                S.wait(sem, val)

    def flush(self):
        nc = self.nc
        if getattr(self, "log", None) is not None:
            for S in self.streams:
                self.log.setdefault(S.name, []).extend(S.ops)

        def run(S, e):
            pend = []
            for o in S.ops:
                if o[0] == "wait":
                    pend.append(o)
                elif o[0] == "op":
                    for w in pend[:-1]:
                        e.wait_ge(w[1], w[2])
                    ins = o[1](e)
                    if pend and FUSE_WAITS:
                        ins._wait_ge(pend[-1][1], pend[-1][2])
                    pend = []
                    if o[2]:
                        ins.then_inc(S.sem, 1)
                else:
                    for w in pend:
                        e.wait_ge(w[1], w[2])
                    pend = []
                    e.dma_start(out=o[1], in_=o[2]).then_inc(o[3], 16)
            for w in pend:
                e.wait_ge(w[1], w[2])
            S.ops = []

        with nc.Block() as block:
            @block.tensor
            def _(e):
                run(self.PE, e)

            @block.scalar
            def _(e):
                run(self.ACT, e)

            @block.vector
            def _(e):
                run(self.DVE, e)

            @block.gpsimd
            def _(e):
                run(self.POOL, e)

            @block.sync
            def _(e):
                run(self.SP, e)


def build_nc():
    _SNAP.clear()
    nc = bass.Bass("TRN2", target_bir_lowering=False)
    din = lambda n, s: nc.dram_tensor(n, s, F32, kind="ExternalInput").ap()
    xT_d = din("xT", [NG, D_MODEL, GT])
    xown_d = din("x_own", [2048, D_MODEL])
    xh_d = din("xh", [D_MODEL, 64])
    pT_d = din("pT", [256, 2048])
    win_d = din("w_in", [D_MODEL, IN_W])
    wpool_d = din("w_pool", [512, 128])
    wout_d = din("w_out", [1024, 1024])
    wpe_d = din("w_pe", [256, 1024])
    wpg_d = din("w_pg", [1024, 1024])
    lamv_d = din("lamv", [1, 256])
    sublng_d = din("subln_g", [128, 1])
    pscale_d = din("pool_scale", [128, 4])
    lng_d = din("ln_g", [1, 1024])
    lnb_d = din("ln_b", [1, 1024])
    bpg_d = din("b_pg", [1, 1024])
    maskb_d = din("maskb", [128, 4])
    cnt_d = din("cnt", [128, 64])
    tri_d = din("tri2", [128, 256])
    ident_d = din("ident", [128, 128])
    out_d = nc.dram_tensor("out", [2048, D_MODEL], F32, kind="ExternalOutput").ap()

    with ExitStack() as top:
        P = Prog(nc, top)
        PE, ACT, DVE, POOL, SP = P.PE, P.ACT, P.DVE, P.POOL, P.SP

        def sb(st, name, shape, dt):
            return st.enter_context(nc.sbuf_tensor("sb_" + name, shape, dt))

        ps = top.enter_context(nc.psum_tensor("psum_all", [128, 8, 512], F32))
        mixT = sb(top, "mixT", [128, 8, 2048], BF16)
        lamv = sb(top, "lamv", [128, 256], F32)
        sublng = sb(top, "sublng", [128, 1], F32)
        pscale = sb(top, "pscale", [128, 4], F32)
        maskb = sb(top, "maskb", [128, 4], F32)
        cnt = sb(top, "cnt", [128, 64], F32)
        tri2 = sb(top, "tri2", [128, 2, 128], BF16)
        ident = sb(top, "ident", [128, 128], BF16)
        identf = sb(top, "identf", [128, 128], F32)
        ones = sb(top, "ones", [128, 128], F32)
        sc = sb(top, "sc", [128, 16], F32)
        ltmp = sb(top, "ltmp", [128, 128], F32)

        B_mix = [[Buf() for _ in range(4)] for _ in range(8)]
        B_const = Buf()
        B_sc = Buf()

        dsc = P.dsem()
        for dst, src in ((lamv[:], lamv_d.partition_broadcast(128)), (sublng[:], sublng_d),
                         (pscale[:], pscale_d), (maskb[:], maskb_d), (cnt[:], cnt_d)):
            P.dma(SP, dst, src, dsc)
        P.dma(SP, identf[:], ident_d, dsc)
        t_const = [(dsc.sem, dsc.cnt)]
        B_ones = Buf()
        P.op(DVE, lambda e: e.memset(ones[:], 1.0), writes=[B_ones])

        B_lt = Buf()
        P.op(DVE, lambda e: e.tensor_tensor(out=ltmp[:, 0:64], in0=lamv[:, 0:64], in1=lamv[:, 64:128], op=ALU.mult),
             writes=[B_lt], deps=t_const)
        P.op(DVE, lambda e: e.tensor_tensor(out=ltmp[:, 64:128], in0=lamv[:, 128:192], in1=lamv[:, 192:256], op=ALU.mult),
             writes=[B_lt], reads=[B_lt])
        P.op(DVE, lambda e: e.reduce_sum(out=sc[:, 0:1], in_=ltmp[:, 0:64], axis=AX.X), reads=[B_lt], writes=[B_sc])
        P.op(DVE, lambda e: e.reduce_sum(out=sc[:, 1:2], in_=ltmp[:, 64:128], axis=AX.X), reads=[B_lt, B_sc], writes=[B_sc])
        P.op(ACT, lambda e: e.activation(out=sc[:, 2:4], in_=sc[:, 0:2], func=AF.Exp), reads=[B_sc], writes=[B_sc])
        P.op(DVE, lambda e: e.tensor_tensor(out=sc[:, 4:5], in0=sc[:, 2:3], in1=sc[:, 3:4], op=ALU.subtract),
             reads=[B_sc], writes=[B_sc])
        P.op(DVE, lambda e: e.tensor_scalar(out=sc[:, 5:6], in0=sc[:, 4:5], scalar1=LAM_INIT, scalar2=-1.0,
                                            op0=ALU.add, op1=ALU.mult), reads=[B_sc], writes=[B_sc])
        P.op(DVE, lambda e: e.tensor_scalar(out=sc[:, 6:7], in0=sublng[:, 0:1],
                                            scalar1=(1.0 - LAM_INIT) * math.sqrt(128.0), scalar2=None, op0=ALU.mult),
             reads=[B_sc], writes=[B_sc], deps=t_const)
        neglam = sc[:, 5:6]
        gsc = sc[:, 6:7]

        with ExitStack() as p12:
            kT = sb(p12, "kT", [128, 4, SEQ], BF16)
            Vt = sb(p12, "Vt", [128, 32, 512], BF16)
            qT = sb(p12, "qT", [128, 4, 2048], BF16)
            B_k = [[Buf() for _ in range(NG)] for _ in range(4)]
            B_v = [Buf() for _ in range(32)]
            B_q = [[Buf() for _ in range(4)] for _ in range(4)]

            with ExitStack() as p1:
                win = sb(p1, "win", [128, 8, IN_W], BF16)
                xb = [sb(p1, f"xb{i}", [128, 8, GT], BF16) for i in range(2)]
                xh = sb(p1, "xh", [128, 8, 64], BF16)
                wpool = sb(p1, "wpool", [128, 4, 128], BF16)
                uh = sb(p1, "uh", [128, 4, 64], F32)
                U = [sb(p1, f"U{i}", [128, 528], F32) for i in range(2)]
                TA = sb(p1, "TA", [128, 528], F32)
                TB = sb(p1, "TB", [128, 528], F32)
                TM = sb(p1, "TM", [128, 512], F32)
                dbf = [sb(p1, f"dbf{i}", [128, 512], BF16) for i in range(4)]
                gps = [sb(p1, f"gps{i}", [128, 512], F32) for i in range(4)]
                B_xb = [Buf(), Buf()]
                B_U = [Buf(), Buf()]
                B_TA, B_TB, B_TM, B_uh = Buf(), Buf(), Buf(), Buf()
                B_d = [Buf() for _ in range(4)]
                B_g = [Buf() for _ in range(4)]
                B_bank = [Buf() for _ in range(8)]
                bank_i = [0]

                def nbank():
                    b = bank_i[0] % 8
                    bank_i[0] += 1
                    return b

                win_v = win_d.rearrange("(kc p) n -> p kc n", p=128)
                B_w = [Buf() for _ in range(6)]
                ds_x = [P.dsem(), P.dsem()]
                P.dma(POOL, xb[0][:], xT_d[0].rearrange("(kc p) t -> p kc t", p=128), ds_x[0], writes=[B_xb[0]])
                B_w1 = [Buf() for _ in range(4)]
                for h in range(4):
                    P.dma(POOL, win[:, :, 512 + h * 128:512 + (h + 1) * 128], win_v[:, :, 512 + h * 128:512 + (h + 1) * 128],
                          P.dsem(), writes=[B_w1[h]])
                dsm = P.dsem()
                for sec in (2, 0, 3, 4, 5):
                    P.dma(POOL, win[:, :, sec * 512:(sec + 1) * 512], win_v[:, :, sec * 512:(sec + 1) * 512],
                          P.dsem(), writes=[B_w[sec]])
                    if sec == 2:
                        P.dma(POOL, xb[1][:], xT_d[1].rearrange("(kc p) t -> p kc t", p=128), ds_x[1], writes=[B_xb[1]])
                    if sec == 4:
                        P.dma(POOL, xh[:], xh_d.rearrange("(kc p) t -> p kc t", p=128), dsm)
                P.dma(POOL, wpool[:], wpool_d.rearrange("(g c) d -> c g d", c=128), dsm)
                t_misc = [(dsm.sem, dsm.cnt)]
                dsc2 = P.dsem()
                P.dma(POOL, tri2[:].rearrange("p c n -> p (c n)"), tri_d, dsc2)
                P.dma(POOL, ident[:], ident_d, dsc2)
                t_const.append((dsc2.sem, dsc2.cnt))

                evac_rr = [0]

                def evac_copy(bank, out_ap, wbufs, scale=None):
                    src = ps[:, bank, :]
                    evac_rr[0] += 1
                    if evac_rr[0] % 2 == 0:
                        if scale is None:
                            P.op(DVE, lambda e: e.tensor_copy(out=out_ap, in_=src), reads=[B_bank[bank]], writes=wbufs)
                        else:
                            P.op(DVE, lambda e: e.tensor_scalar(out=out_ap, in0=src, scalar1=scale, scalar2=None,
                                                                op0=ALU.mult), reads=[B_bank[bank]], writes=wbufs)
                    else:
                        P.op(ACT, lambda e: e.activation(out=out_ap, in_=src, func=AF.Copy,
                                                         scale=(1.0 if scale is None else scale)),
                             reads=[B_bank[bank]], writes=wbufs)

                def proj_chunk(fc, xbuf, bxb):
                    bank = nbank()
                    sec = fc // 4
                    bw = B_w1[fc - 4] if sec == 1 else B_w[sec]
                    for kc in range(8):
                        P.op(PE, lambda e, kc=kc: e.matmul(ps[:, bank, :], win[:, kc, fc * 128:(fc + 1) * 128],
                                                          xbuf[:, kc, :], start=(kc == 0), stop=(kc == 7)),
                             reads=[bw, bxb], writes=[B_bank[bank]], sig=(kc == 7))
                    return bank

                def halo():
                    for p in range(4):
                        bank = nbank()
                        fc = 16 + p
                        for kc in range(8):
                            P.op(PE, lambda e, kc=kc, fc=fc, bank=bank: e.matmul(
                                ps[:, bank, 0:64], win[:, kc, fc * 128:(fc + 1) * 128], xh[:, kc, :],
                                start=(kc == 0), stop=(kc == 7)),
                                reads=[B_w[4]], writes=[B_bank[bank]], sig=(kc == 7), deps=t_misc)
                        P.op(DVE, lambda e, p=p, bank=bank: e.tensor_copy(out=uh[:, p, :], in_=ps[:, bank, 0:64]),
                             reads=[B_bank[bank]], writes=[B_uh])

                deferred = []

                ORDER = [0, 1, 2, 3, 4, 5, 7, 6]

                def do_group(s, pos):
                    if pos == NG - 1:
                        P.op(ACT, lambda e: e.activation(out=sc[:, 8:9], in_=sc[:, 0:1], func=AF.Exp), reads=[B_sc], writes=[B_sc])
                    own = (s % 2 == 1)
                    j = s // 2
                    xbuf, bxb = xb[pos % 2], B_xb[pos % 2]
                    if 1 <= pos and pos + 1 < NG:
                        P.dma(POOL, xb[(pos + 1) % 2][:], xT_d[ORDER[pos + 1]].rearrange("(kc p) t -> p kc t", p=128),
                              ds_x[(pos + 1) % 2], writes=[B_xb[(pos + 1) % 2]])
                    for h in range(4):
                        bank = proj_chunk(4 + h, xbuf, bxb)
                        evac_copy(bank, kT[:, h, s * GT:(s + 1) * GT], [B_k[h][s]])
                    for f in deferred:
                        f()
                    del deferred[:]
                    for tb in range(4):
                        bank = nbank()
                        for kc in range(8):
                            P.op(PE, lambda e, o=ps[:, bank, :], a=xbuf[:, kc, tb * 128:(tb + 1) * 128], b=win[:, kc, 1024:1536],
                                 st_=(kc == 0), sp_=(kc == 7): e.matmul(o, a, b, start=st_, stop=sp_),
                                 reads=[B_w[2], bxb], writes=[B_bank[bank]], sig=(kc == 7))
                        evac_copy(bank, Vt[:, s * 4 + tb, :], [B_v[s * 4 + tb]])
                    if not own:
                        return
                    tok = slice(j * GT, (j + 1) * GT)
                    for h in range(4):
                        bank = proj_chunk(h, xbuf, bxb)
                        evac_copy(bank, qT[:, h, tok], [B_q[h][j]], scale=0.125)
                    for h in range(4):
                        bank = proj_chunk(12 + h, xbuf, bxb)
                        P.op(ACT, lambda e, o=mixT[:, h, tok], i=ps[:, bank, :]: e.activation(out=o, in_=i, func=AF.Silu),
                             reads=[B_bank[bank]], writes=[B_mix[h][j]])
                    if s == 1:
                        halo()
                    for p in range(4):
                        w = POOL_W[p]
                        Ub, bU = U[p % 2], B_U[p % 2]
                        bank = proj_chunk(16 + p, xbuf, bxb)
                        P.op(DVE, lambda e, o=Ub[:, 16:528], i=ps[:, bank, :]: e.tensor_copy(out=o, in_=i),
                             reads=[B_bank[bank]], writes=[bU])
                        P.op(DVE, lambda e, o=Ub[:, 0:16], i=uh[:, p, j * 16:(j + 1) * 16]: e.tensor_copy(out=o, in_=i),
                             reads=[B_uh, bU], writes=[bU])
                        bank = proj_chunk(20 + p, xbuf, bxb)
                        P.op(ACT, lambda e, o=gps[p][:], i=ps[:, bank, :]: e.activation(out=o, in_=i, func=AF.Silu),
                             reads=[B_bank[bank]], writes=[B_g[p]])
                        srcs = [(Ub, bU)]
                        lv = [(TA, B_TA), (TB, B_TB)]
                        for l in range(p + 1):
                            sh = 1 << l
                            lo = (1 << (l + 1)) - 1
                            src, bsrc = srcs[-1]
                            dst, bdst = lv[l % 2]
                            P.op(DVE, lambda e, o=dst[:, lo:528], a=src[:, lo:528], b=src[:, lo - sh:528 - sh]:
                                 e.tensor_tensor(out=o, in0=a, in1=b, op=ALU.add),
                                 reads=[bsrc], writes=[bdst])
                            srcs.append((dst, bdst))
                        S_w, bS = srcs[-1]
                        P.op(DVE, lambda e, o=dbf[p][:], a=S_w[:, 16:528], b=Ub[:, 16:528], w=w:
                             e.scalar_tensor_tensor(out=o, in0=a, scalar=1.0 / w, in1=b, op0=ALU.mult, op1=ALU.subtract),
                             reads=[bS, bU], writes=[B_d[p]])
                        if j == 0:
                            P.op(DVE, lambda e, a=S_w[:, 16:32], b=cnt[:, p * 16:(p + 1) * 16]:
                                 e.tensor_tensor(out=TM[:, 0:16], in0=a, in1=b, op=ALU.mult),
                                 reads=[bS, B_TM], writes=[B_TM], deps=t_const)
                            P.op(DVE, lambda e, o=dbf[p][:, 0:16], b=Ub[:, 16:32]:
                                 e.tensor_tensor(out=o, in0=TM[:, 0:16], in1=b, op=ALU.subtract),
                                 reads=[B_TM, bU, B_d[p]], writes=[B_d[p]])

                        def ymm(p=p, tok=tok, j=j):
                            bank = nbank()
                            P.op(PE, lambda e: e.matmul(ps[:, bank, :], wpool[:, p, :], dbf[p][:], start=True, stop=True),
                                 reads=[B_d[p]], writes=[B_bank[bank]], deps=t_misc)
                            P.op(DVE, lambda e: e.scalar_tensor_tensor(out=mixT[:, 4 + p, tok], in0=ps[:, bank, :],
                                                                       scalar=pscale[:, p:p + 1], in1=gps[p][:],
                                                                       op0=ALU.mult, op1=ALU.mult),
                                 reads=[B_bank[bank], B_g[p]], writes=[B_mix[4 + p][j]], deps=t_const)
                        deferred.append(ymm)

                for pos, s in enumerate(ORDER):
                    do_group(s, pos)
                for f in deferred:
                    f()
                P.barrier()

            pw3 = ExitStack()
            p12.enter_context(pw3)
            wout = sb(pw3, "wout", [128, 8, 1024], BF16)
            wpg = sb(pw3, "wpg", [128, 8, 1024], BF16)
            wpe = sb(pw3, "wpe", [128, 2, 1024], BF16)
            pTb = sb(pw3, "pTb", [128, 2, 2048], BF16)
            st6 = sb(pw3, "st6", [128, 12], F32)
            mv = [sb(pw3, f"mv{i}", [128, 8], F32) for i in range(2)]
            B_wo, B_wg, B_we, B_pT = Buf(), Buf(), Buf(), Buf()
            P.dma(POOL, wout[:], wout_d.rearrange("(kc p) n -> p kc n", p=128), P.dsem(), writes=[B_wo])
            P.dma(POOL, wpg[:], wpg_d.rearrange("(kc p) n -> p kc n", p=128), P.dsem(), writes=[B_wg])
            P.dma(POOL, pTb[:], pT_d.rearrange("(kc p) t -> p kc t", p=128), P.dsem(), writes=[B_pT])
            P.dma(POOL, wpe[:], wpe_d.rearrange("(kc p) n -> p kc n", p=128), P.dsem(), writes=[B_we])

            if True:
                p2 = p12.enter_context(ExitStack())
                NPT = 6
                PT = [sb(p2, f"PT{i}", [128, 2, 512], BF16) for i in range(NPT)]
                Lacc = [sb(p2, f"Lacc{i}", [128, 2, 512], F32) for i in range(2)]
                ones_bf = sb(p2, "ones_bf", [128, 128], BF16)
                Lh = sb(p2, "Lh", [128, 2, 512], BF16)
                Ll = sb(p2, "Ll", [128, 2, 512], BF16)
                sqh = sb(p2, "sqh", [128, 512], BF16)
                sql = sb(p2, "sql", [128, 512], BF16)
                B_Lh, B_Ll, B_sqh, B_sql = Buf(), Buf(), Buf(), Buf()
                o12s = sb(p2, "o12s", [128, 2, 512], F32)
                rr = sb(p2, "rr", [128, 2, 512], F32)
                lnl = rr
                t1 = sb(p2, "t1", [128, 512], F32)
                t2 = sb(p2, "t2", [128, 512], F32)
                oo = sb(p2, "oo", [128, 512], F32)
                sq = sb(p2, "sq", [128, 512], F32)
                rstd = sb(p2, "rstd", [128, 512], F32)
                lnm = rstd
                orn = sb(p2, "orn", [128, 512], F32)
                B_PT = [Buf() for _ in range(NPT)]
                B_L = [Buf(), Buf()]
                B_S = [Buf(), Buf()]
                B_O = Buf()
                B_Lp = Buf()
                B_o1s, B_o2s, B_onesb = Buf(), Buf(), Buf()
                B_lnl, B_rr, B_t1, B_t2, B_oo, B_sq, B_lnm, B_rstd, B_orn = (Buf() for _ in range(9))
                sslot = [0]
                P.op(DVE, lambda e: e.memset(ones_bf[:], 1.0), writes=[B_onesb])

                def nslot():
                    v = sslot[0] % 2
                    sslot[0] += 1
                    return v

                its = []
                for j in range(4):
                    for h in range(4):
                        ng = 2 * j + 2
                        for s in range(ng):
                            for kb in range(4):
                                its.append(dict(j=j, h=h, s=s, kb=kb, first=(s == 0 and kb == 0),
                                                last=(s == ng - 1 and kb == 3), jh=j * 4 + h, li=s * 4 + kb))
                state = {}

                def qk(i):
                    it = its[i]
                    j, h, s, kb = it["j"], it["h"], it["s"], it["kb"]
                    diag = (s == 2 * j + 1)
                    q0 = 128 * kb if diag else 0
                    sl = nslot()
                    key0 = s * GT + kb * 128
                    for c in range(2):
                        P.op(PE, lambda e, c=c: e.matmul(ps[:, 2 * sl + c, q0:512], kT[64 * c:64 * c + 64, h, key0:key0 + 128],
                                                        qT[64 * c:64 * c + 64, h, j * GT + q0:(j + 1) * GT],
                                                        start=True, stop=(not diag)),
                             reads=[B_k[h][s], B_q[h][j]], writes=[B_S[sl]], sig=(c == 1 and not diag))
                    if diag:
                        for c in range(2):
                            P.op(PE, lambda e, c=c: e.matmul(ps[:, 2 * sl + c, q0:q0 + 128], ident[:], tri2[:, c, :],
                                                            start=False, stop=True),
                                 writes=[B_S[sl]], sig=(c == 1), deps=t_const)
                    pi = i % NPT
                    if s == 2 * j:
                        P.op(ACT, lambda e: e.activation(out=PT[pi][:, :, q0:512], in_=ps[:, 2 * sl:2 * sl + 2, q0:512],
                                                         func=AF.Exp, bias=maskb[:, j:j + 1], scale=1.0),
                             reads=[B_S[sl]], writes=[B_PT[pi]], deps=t_const)
                    else:
                        P.op(ACT, lambda e: e.activation(out=PT[pi][:, :, q0:512], in_=ps[:, 2 * sl:2 * sl + 2, q0:512],
                                                         func=AF.Exp),
                             reads=[B_S[sl]], writes=[B_PT[pi]])
                    lb = it["jh"] % 2
                    li = it["li"]
                    on_pe = li >= 3 and ((li % 2 == 1) if j <= 1 else (li % 3 == 2))
                    if on_pe:
                        pass
                    elif it["first"]:
                        P.op(DVE, lambda e: e.tensor_copy(out=Lacc[lb][:], in_=PT[pi][:]), reads=[B_PT[pi]], writes=[B_L[lb]])
                    else:
                        P.op(DVE, lambda e: e.tensor_tensor(out=Lacc[lb][:, :, q0:512], in0=Lacc[lb][:, :, q0:512],
                                                            in1=PT[pi][:, :, q0:512], op=ALU.add),
                             reads=[B_PT[pi], B_L[lb]], writes=[B_L[lb]])
                    if it["last"] and j > 0:
                        P.op(DVE, lambda e: e.tensor_copy(out=Lh[:], in_=Lacc[lb][:]), reads=[B_L[lb]], writes=[B_Lh])
                        P.op(DVE, lambda e: e.tensor_tensor(out=Ll[:], in0=Lacc[lb][:], in1=Lh[:], op=ALU.subtract),
                             reads=[B_L[lb], B_Lh], writes=[B_Ll])
                    state[i] = (q0, pi, on_pe)

                def av(i):
                    it = its[i]
                    j, h, s, kb = it["j"], it["h"], it["s"], it["kb"]
                    q0, pi, on_pe = state.pop(i)
                    for c in range(2):
                        P.op(PE, lambda e, c=c: e.matmul(ps[:, 4 + c, q0:512], Vt[:, s * 4 + kb, h * 128:(h + 1) * 128],
                                                        PT[pi][:, c, q0:512], start=it["first"], stop=it["last"]),
                             reads=[B_PT[pi], B_v[s * 4 + kb]], writes=[B_O], sig=(c == 1))
                    if on_pe:
                        for c in range(2):
                            P.op(PE, lambda e, c=c: e.matmul(ps[:, 6 + c, q0:512], ones_bf[:], PT[pi][:, c, q0:512],
                                                            start=(it["li"] == (3 if j <= 1 else 5)), stop=False),
                                 reads=[B_PT[pi], B_onesb], writes=[B_Lp], sig=(c == 1))

                def fin0(jh):
                    P.op(ACT, lambda e: e.activation(out=o12s[:], in_=ps[:, 4:6, :], func=AF.Copy), reads=[B_O], writes=[B_o1s, B_o2s])

                def fin1a(jh):
                    if jh < 4:
                        lb = jh % 2
                        for c in range(2):
                            P.op(PE, lambda e, c=c: e.matmul(ps[:, 6 + c, :], ones[:], Lacc[lb][:, c, :], start=False, stop=True),
                                 reads=[B_L[lb], B_ones], writes=[B_Lp], sig=(c == 1))
                        return
                    for c in range(2):
                        P.op(PE, lambda e, c=c: e.matmul(ps[:, 6 + c, :], ones_bf[:], Lh[:, c, :], start=False, stop=False),
                             reads=[B_Lh, B_onesb], writes=[B_Lp], sig=False)
                        P.op(PE, lambda e, c=c: e.matmul(ps[:, 6 + c, :], ones_bf[:], Ll[:, c, :], start=False, stop=True),
                             reads=[B_Ll, B_onesb], writes=[B_Lp], sig=(c == 1))

                def fin1b0(jh):
                    P.op(ACT, lambda e: e.activation(out=lnl[:], in_=ps[:, 6:8, :], func=AF.Ln),
                         reads=[B_Lp], writes=[B_lnl, B_rr])

                def fin1b(jh):
                    P.op(ACT, lambda e: e.activation(out=rr[:], in_=lnl[:], func=AF.Exp, scale=-1.0),
                         reads=[B_lnl], writes=[B_rr, B_lnl])
                    P.op(DVE, lambda e: e.scalar_tensor_tensor(out=t2[:], in0=o12s[:, 1, :], scalar=neglam,
                                                               in1=rr[:, 1, :], op0=ALU.mult, op1=ALU.mult),
                         reads=[B_o2s, B_rr, B_sc], writes=[B_t2])
                    P.op(DVE, lambda e: e.tensor_tensor(out=t1[:], in0=o12s[:, 0, :], in1=rr[:, 0, :], op=ALU.mult),
                         reads=[B_o1s, B_rr], writes=[B_t1])
                    P.op(DVE, lambda e: e.tensor_tensor(out=oo[:], in0=t1[:], in1=t2[:], op=ALU.add),
                         reads=[B_t1, B_t2], writes=[B_oo])
                    P.op(DVE, lambda e: e.tensor_tensor(out=sq[:], in0=oo[:], in1=oo[:], op=ALU.mult),
                         reads=[B_oo], writes=[B_sq])
                    P.op(DVE, lambda e: e.tensor_copy(out=sqh[:], in_=sq[:]), reads=[B_sq], writes=[B_sqh])
                    P.op(DVE, lambda e: e.tensor_tensor(out=sql[:], in0=sq[:], in1=sqh[:], op=ALU.subtract),
                         reads=[B_sq, B_sqh], writes=[B_sql])

                def fin2(jh):
                    j, h = jh // 4, jh % 4
                    tok = slice(j * GT, (j + 1) * GT)
                    sl = nslot()
                    P.op(PE, lambda e: e.matmul(ps[:, 2 * sl, :], ones_bf[:], sqh[:], start=True, stop=False),
                         reads=[B_sqh, B_onesb], writes=[B_S[sl]], sig=False)
                    P.op(PE, lambda e: e.matmul(ps[:, 2 * sl, :], ones_bf[:], sql[:], start=False, stop=True),
                         reads=[B_sql, B_onesb], writes=[B_S[sl]], sig=True)
                    P.op(ACT, lambda e: e.activation(out=lnm[:], in_=ps[:, 2 * sl, :], func=AF.Ln, bias=epsb, scale=1.0),
                         reads=[B_S[sl], B_eps], writes=[B_lnm, B_rstd])
                    P.op(ACT, lambda e: e.activation(out=rstd[:], in_=lnm[:], func=AF.Exp, scale=-0.5),
                         reads=[B_lnm], writes=[B_rstd, B_lnm])
                    P.op(DVE, lambda e: e.tensor_tensor(out=orn[:], in0=oo[:], in1=rstd[:], op=ALU.mult),
                         reads=[B_oo, B_rstd], writes=[B_orn])
                    P.op(DVE, lambda e: e.scalar_tensor_tensor(out=mixT[:, h, tok], in0=orn[:], scalar=gsc,
                                                               in1=mixT[:, h, tok], op0=ALU.mult, op1=ALU.mult),
                         reads=[B_orn, B_sc, B_mix[h][j]], writes=[B_mix[h][j]])

                epsb = sc[:, 7:8]
                B_eps = Buf()
                P.op(DVE, lambda e: e.memset(sc[:, 7:8], 128.0 * LN_EPS), writes=[B_eps])

                class _V:
                    def __init__(self, ap):
                        self.ap = ap

                    def __getitem__(self, k):
                        return self.ap[k]

                def vt_f32(k):
                    return _V(Vt[:, 4 * k:4 * k + 4, :].rearrange("p a n -> p (a n)").bitcast(F32).rearrange("p (a n) -> p a n", a=2))

                def q_f32(h):
                    return _V(qT[:, h, :].bitcast(F32).rearrange("p (a n) -> p a n", a=2))

                def k_f32(h, half):
                    return _V(kT[:, h, half * 2048:(half + 1) * 2048].bitcast(F32).rearrange("p (a n) -> p a n", a=2))

                lngb, lnbb, bpgb = k_f32(0, 0), k_f32(0, 1), k_f32(1, 0)
                xt = [k_f32(1, 1), k_f32(2, 0)]
                yy, yn = vt_f32(2), vt_f32(3)
                hh = [vt_f32(4), vt_f32(5)]
                gsum, gate = vt_f32(6), vt_f32(7)
                pg = q_f32(0)
                ot = [q_f32(1), q_f32(2)]
                hT = [_V(kT[:, 2, 2048:3072]), _V(kT[:, 2, 3072:4096])]
                B_xt = [Buf(), Buf()]
                B_hh = [Buf(), Buf()]
                B_hbf2 = [Buf(), Buf()]
                B_hT = [[Buf(), Buf()], [Buf(), Buf()]]
                B_ot = [Buf(), Buf()]
                B_mv = [Buf(), Buf()]
                B_rs = [Buf(), Buf()]
                B_mc = [Buf(), Buf()]
                B_yy, B_yn, B_gsum, B_gate, B_pg, B_st6 = (Buf() for _ in range(6))
                B_pm, B_pt, B_pgt, B_ppe = B_S[0], B_S[1], B_O, B_Lp
                t_alias = []

                dsv = P.dsem()
                t_vec = []
                ds_xt = [P.dsem(), P.dsem()]
                ds_ot = [P.dsem(), P.dsem()]
                NTB = 16

                def load_x(tb, deps=None):
                    P.dma(SP, xt[tb % 2][:].rearrange("p a n -> p (a n)"), xown_d[tb * 128:(tb + 1) * 128, :], ds_xt[tb % 2],
                          writes=[B_xt[tb % 2]], deps=(t_alias if deps is None else deps))

                for i in range(2):
                    P.op(DVE, lambda e, i=i: e.memset(mv[i][:, 5:6], -0.5), writes=[B_mc[i]])

                def tsl(t):
                    return slice(t * 128, (t + 1) * 128)

                def PE_M(t):
                    for nh in range(2):
                        for kc in range(8):
                            P.op(PE, lambda e, nh=nh, kc=kc: e.matmul(ps[:, nh, :], mixT[:, kc, tsl(t)], wout[:, kc, nh * 512:(nh + 1) * 512],
                                                                     start=(kc == 0), stop=(kc == 7)),
                                 reads=[B_wo, B_mix[kc][t // 4]], writes=[B_pm], sig=(kc == 7 and nh == 1))

                def DVE_Y(t):
                    P.op(DVE, lambda e: e.scalar_tensor_tensor(out=yy[:], in0=xt[t % 2][:], scalar=ALPHA, in1=ps[:, 0:2, :],
                                                               op0=ALU.mult, op1=ALU.add),
                         reads=[B_xt[t % 2], B_pm], writes=[B_yy], deps=t_alias)
                    if t + 2 < NTB:
                        load_x(t + 2)
                    P.op(DVE, lambda e: e.bn_stats(out=st6[:, 0:6], in_=yy[:, 0, :]), reads=[B_yy], writes=[B_st6])
                    P.op(DVE, lambda e: e.bn_stats(out=st6[:, 6:12], in_=yy[:, 1, :]), reads=[B_yy, B_st6], writes=[B_st6])
                    P.op(DVE, lambda e: e.bn_aggr(out=mv[t % 2][:, 0:2], in_=st6[:, 0:12]), reads=[B_st6, B_mv[t % 2]], writes=[B_mv[t % 2]])
                    P.op(DVE, lambda e: e.scalar_tensor_tensor(out=yn[:], in0=yy[:], scalar=mv[t % 2][:, 0:1], in1=lngb[:],
                                                               op0=ALU.subtract, op1=ALU.mult),
                         reads=[B_yy, B_mv[t % 2]], writes=[B_yn], deps=t_vec + t_alias)

                def POOLW3(t):
                    m = mv[t % 2]
                    P.op(POOL, lambda e: e.tensor_scalar(out=m[:, 2:3], in0=m[:, 1:2], scalar1=LN_EPS, scalar2=None, op0=ALU.add),
                         reads=[B_mv[t % 2]], writes=[B_rs[t % 2]])
                    P.op(POOL, lambda e: e.tensor_tensor(out=m[:, 3:4], in0=m[:, 2:3], in1=m[:, 5:6], op=ALU.pow),
                         reads=[B_rs[t % 2], B_mc[t % 2]], writes=[B_rs[t % 2]])

                def DVE_H(t):
                    m = mv[t % 2]
                    P.op(DVE, lambda e: e.scalar_tensor_tensor(out=hh[t % 2][:], in0=yn[:], scalar=m[:, 3:4], in1=lnbb[:],
                                                               op0=ALU.mult, op1=ALU.add),
                         reads=[B_yn, B_rs[t % 2]], writes=[B_hh[t % 2]], deps=t_alias)

                psT = ps[:, 2:4, :]

                def PE_T(t):
                    hflat = hh[t % 2][:].rearrange("p a n -> p (a n)")
                    for kc in range(8):
                        P.op(PE, lambda e, kc=kc: e.transpose(ps[:, 2 + kc // 4, (kc % 4) * 128:(kc % 4 + 1) * 128],
                                                              hflat[:, kc * 128:(kc + 1) * 128], identf[:]),
                             reads=[B_hh[t % 2]], writes=[B_pt], sig=(kc == 7), deps=t_const)

                def ACT_E(t):
                    for hf in range(2):
                        P.op(ACT, lambda e, hf=hf: e.activation(out=hT[t % 2][:, hf * 512:(hf + 1) * 512], in_=ps[:, 2 + hf, :], func=AF.Copy),
                             reads=[B_pt], writes=[B_hT[t % 2][hf]], deps=t_alias)

                def PE_pe(t):
                    for nh in range(2):
                        for kc in range(2):
                            P.op(PE, lambda e, nh=nh, kc=kc: e.matmul(ps[:, 6 + nh, :], pTb[:, kc, tsl(t)], wpe[:, kc, nh * 512:(nh + 1) * 512],
                                                                     start=(kc == 0), stop=(kc == 1)),
                                 reads=[B_we, B_pT], writes=[B_ppe], sig=(kc == 1 and nh == 1))

                def PE_G(t):
                    hTb, bT = hT[t % 2], B_hT[t % 2]
                    PE_pe(t)
                    if t == NTB - 1:
                        PE_pe(t)
                        PE_pe(t)
                    for hf in range(2):
                        for nh in range(2):
                            for kc in range(4 * hf, 4 * hf + 4):
                                P.op(PE, lambda e, nh=nh, kc=kc: e.matmul(ps[:, 4 + nh, :], hTb[:, kc * 128:(kc + 1) * 128],
                                                                         wpg[:, kc, nh * 512:(nh + 1) * 512],
                                                                         start=(kc == 0), stop=(kc == 7)),
                                     reads=[B_wg, bT[hf]], writes=[B_pgt], sig=(kc == 7 and nh == 1))

                def DVE_S(t):
                    P.op(DVE, lambda e: e.tensor_tensor(out=gsum[:], in0=ps[:, 4:6, :], in1=bpgb[:], op=ALU.add),
                         reads=[B_pgt], writes=[B_gsum], deps=t_vec + t_alias)

                def ACT_Z(t):
                    P.op(ACT, lambda e: e.activation(out=gate[:], in_=gsum[:], func=AF.Sigmoid), reads=[B_gsum], writes=[B_gate], deps=t_alias)

                def DVE_Q(t):
                    P.op(DVE, lambda e: e.tensor_tensor(out=pg[:], in0=ps[:, 6:8, :], in1=gate[:], op=ALU.mult),
                         reads=[B_ppe, B_gate], writes=[B_pg], deps=t_alias)

                def POOL_O(t):
                    P.op(DVE if t == NTB - 1 else POOL, lambda e: e.tensor_tensor(out=ot[t % 2][:], in0=pg[:], in1=hh[t % 2][:], op=ALU.add),
                         reads=[B_pg, B_hh[t % 2]], writes=[B_ot[t % 2]], deps=t_alias)
                    P.dma(SP, out_d[tsl(t), :], ot[t % 2][:].rearrange("p a n -> p (a n)"), ds_ot[t % 2], reads=[B_ot[t % 2]])


                n = len(its)
                sched = {}

                def at(step, f):
                    sched.setdefault(step, []).append(f)

                step = 0
                while step <= n + 1:
                    if step < n:
                        qk(step)
                        if step == n - 33:
                            t_early = [(PE.sem, PE.cnt)]
                            for dst, src in ((lngb, lng_d), (lnbb, lnb_d), (bpgb, bpg_d)):
                                P.dma(SP, dst[:].rearrange("p a n -> p (a n)"), src.partition_broadcast(128), dsv, deps=t_early)
                            t_vec.append((dsv.sem, dsv.cnt))
                            load_x(0, t_early)
                            load_x(1, t_early)
                    for f in sched.pop(step, []):
                        f()
                    if 2 <= step <= n + 1:
                        av(step - 2)
                        if its[step - 2]["last"]:
                            jh = its[step - 2]["jh"]
                            fin0(jh)
                            if step <= n:
                                if jh < 0:
                                    at(step + 1, lambda jh=jh: fin1a(jh))
                                    at(step + 2, lambda jh=jh: fin1b0(jh))
                                    at(step + 3, lambda jh=jh: fin1b(jh))
                                    at(step + 8, lambda jh=jh: fin2(jh))
                                else:
                                    fin1a(jh)
                                    at(step + 1, lambda jh=jh: fin1b0(jh))
                                    at(step + 2, lambda jh=jh: fin1b(jh))
                                    at(step + 8, lambda jh=jh: fin2(jh))
                    step += 1
                assert not sched, sorted(sched)
                t_alias.append((PE.sem, PE.cnt))
                PE_M(0); DVE_Y(0); POOLW3(0); DVE_H(0)
                fin1a(15)
                fin1b0(15)
                fin1b(15)
                PE_M(1); PE_T(0); ACT_E(0); DVE_Y(1); POOLW3(1); PE_G(0)
                fin2(15)
                DVE_S(0); ACT_Z(0); DVE_H(1); DVE_Q(0); POOL_O(0)

            if True:
                for t in range(1, NTB):
                    nx = t + 1 < NTB
                    if nx:
                        PE_M(t + 1)
                    PE_T(t)
                    ACT_E(t)
                    if nx:
                        DVE_Y(t + 1)
                        POOLW3(t + 1)
                    PE_G(t)
                    DVE_S(t)
                    ACT_Z(t)
                    if nx:
                        DVE_H(t + 1)
                    DVE_Q(t)
                    POOL_O(t)
                for d in ds_ot:
                    SP.wait(d.sem, d.cnt)
                P.barrier([(d.sem, d.cnt) for d in ds_ot])
                P.flush()
    return nc


_NC_CACHE = {}


def _host_inputs(x, p, w_in, lam_q1, lam_k1, lam_q2, lam_k2, subln_g, w_pool, pool_scale,
                 w_out, ln_g, ln_b, w_pe, w_pg, b_pg):
    f = lambda a: np.ascontiguousarray(np.asarray(a, dtype=np.float32))
    x, p = f(x), f(p)
    common = {
        "w_in": f(w_in)[0], "w_pool": f(w_pool)[0].reshape(512, 128), "w_out": f(w_out)[0],
        "w_pe": f(w_pe)[0], "w_pg": f(w_pg)[0],
        "lamv": np.concatenate([f(lam_q1)[0], f(lam_k1)[0], f(lam_q2)[0], f(lam_k2)[0]]).reshape(1, 256),
        "subln_g": f(subln_g)[0].reshape(128, 1),
        "pool_scale": np.ascontiguousarray(f(pool_scale)[0].reshape(4, 128).T),
        "ln_g": f(ln_g)[0].reshape(1, 1024), "ln_b": f(ln_b)[0].reshape(1, 1024), "b_pg": f(b_pg)[0].reshape(1, 1024),
        "ident": np.eye(128, dtype=np.float32),
    }
    kk = np.arange(128)[:, None]
    qq = np.arange(128)[None, :]
    tri = np.where(qq >= kk, 0.0, MASK_NEG).astype(np.float32)
    common["tri2"] = np.ascontiguousarray(np.concatenate([tri, tri], axis=1))
    in_maps = []
    for c in range(8):
        b, r = c // 2, c % 2
        O = OWN[r]
        X = OWN[1 - r]
        order = [X[0], O[0], X[1], O[1], X[2], O[2], X[3], O[3]]
        xg = x[b].reshape(NG, GT, D_MODEL)
        xT = np.ascontiguousarray(xg[order].transpose(0, 2, 1))
        x_own = np.ascontiguousarray(xg[O].reshape(4 * GT, D_MODEL))
        xh = np.zeros((4, 16, D_MODEL), np.float32)
        for j, g in enumerate(O):
            if g > 0:
                xh[j] = x[b, g * GT - 16:g * GT, :]
        xh = np.ascontiguousarray(xh.reshape(64, D_MODEL).T)
        pT = np.ascontiguousarray(p[0, b].reshape(NG, GT, 256)[O].reshape(4 * GT, 256).T)
        maskb = np.zeros((128, 4), np.float32)
        for j in range(4):
            if (r == 0 and j % 2 == 0) or (r == 1 and j % 2 == 1):
                maskb[:, j] = MASK_NEG
        cnt = np.zeros((128, 4, 16), np.float32)
        t = np.arange(16, dtype=np.float32)
        for pi, w in enumerate(POOL_W):
            if O[0] == 0:
                cnt[:, pi, :] = 1.0 / np.minimum(t + 1.0, float(w))
            else:
                cnt[:, pi, :] = 1.0 / float(w)
        m = dict(common)
        m.update({"xT": xT, "x_own": x_own, "xh": xh, "pT": pT, "maskb": maskb, "cnt": cnt.reshape(128, 64)})
        in_maps.append(m)
    return in_maps


def kernel(**inputs):
    in_maps = _host_inputs(**inputs)
    if "nc" not in _NC_CACHE:
        _NC_CACHE["nc"] = build_nc()
    nc = _NC_CACHE["nc"]
    res = run_bass_kernel_spmd(nc, in_maps, core_ids=list(range(8)))
    out = np.empty((4, SEQ, D_MODEL), np.float32)
    for c in range(8):
        b, r = c // 2, c % 2
        o = np.asarray(res.results[c]["out"], dtype=np.float32).reshape(4, GT, D_MODEL)
        for j, g in enumerate(OWN[r]):
            out[b, g * GT:(g + 1) * GT, :] = o[j]
    return out
```
